# Optimizing a Trainium2 kernel written in Bass

```python
import math, functools
import jax, jax.numpy as jnp
from jax import lax
import numpy as np

D_MODEL = 1024
BATCH = 4
SEQ = 4096
DEPTH = 2
DEC_BATCH = 32
DEC_SEQ = 1
PAST_LEN = 8192
PAGE_SIZE = 128

N_ATT_HEADS = 4
QK_DIM = 64
V_DIM = 2 * QK_DIM
W_ATT = N_ATT_HEADS * V_DIM
W_CONV = 512
CONV_WIDTH = 3
CHUNK = 128
N_CHUNK_GROUPS = 4
W_CHUNK = 512
CHUNK_GROUP_DIM = W_CHUNK // N_CHUNK_GROUPS
N_BRANCH = 3
BRANCH_W = 512
Q_BLOCK = 128
ALPHA = (2.0 * DEPTH) ** 0.25
BETA = (8.0 * DEPTH) ** -0.25
LN_EPS = 1e-5
ATT_SCALE = QK_DIM ** -0.5
COL_SIZES = (N_ATT_HEADS * 2 * QK_DIM, N_ATT_HEADS * 2 * QK_DIM, W_ATT, W_ATT,
             W_CONV, W_CONV, W_CONV, W_CONV,
             W_CHUNK, W_CHUNK, W_CHUNK,
             N_BRANCH * D_MODEL)
COL_SPLITS = tuple(int(i) for i in np.cumsum(COL_SIZES)[:-1])
D_IN = int(sum(COL_SIZES))

kernel_name = "hybrid_diffattn_shortconv_chunkmlp_step"


def layer_norm(x, g, b):
    xf = x.astype(jnp.float32)
    mu = jnp.mean(xf, axis=-1, keepdims=True)
    var = jnp.mean(jnp.square(xf - mu), axis=-1, keepdims=True)
    y = (xf - mu) * lax.rsqrt(var + LN_EPS)
    return (y * g.astype(jnp.float32) + b.astype(jnp.float32)).astype(x.dtype)


def rms_norm(x, g):
    xf = x.astype(jnp.float32)
    y = xf * lax.rsqrt(jnp.mean(jnp.square(xf), axis=-1, keepdims=True) + LN_EPS)
    return (y * g.astype(jnp.float32)).astype(x.dtype)


def lambda_init(l):
    return 0.8 - 0.6 * math.exp(-0.3 * l)


def diff_lambda(lam_l, l):
    lf = lam_l.astype(jnp.float32)
    return jnp.exp(jnp.sum(lf[0] * lf[1])) - jnp.exp(jnp.sum(lf[2] * lf[3])) + lambda_init(l)


def diff_attn_prompt(q, k, v, lam):
    b_, s_ = q.shape[0], q.shape[1]
    nb = s_ // Q_BLOCK
    qb = q.reshape(b_, nb, Q_BLOCK, N_ATT_HEADS, 2, QK_DIM).swapaxes(0, 1)
    key_pos = jnp.arange(s_)

    def block(args):
        q_blk, i = args
        s = jnp.einsum('bqhmd,bkhmd->bhmqk', q_blk, k).astype(jnp.float32) * ATT_SCALE
        q_pos = i * Q_BLOCK + jnp.arange(Q_BLOCK)
        mask = key_pos[None, :] <= q_pos[:, None]
        p = jax.nn.softmax(jnp.where(mask, s, -jnp.inf), axis=-1)
        a = p[:, :, 0] - lam * p[:, :, 1]
        return jnp.einsum('bhqk,bkhd->bqhd', a.astype(v.dtype), v)

    o = lax.map(block, (qb, jnp.arange(nb)))
    return o.swapaxes(0, 1).reshape(b_, s_, N_ATT_HEADS, V_DIM)


def diff_attn_sample(q, k, v, lam, k_past, v_past):
    t = q.shape[1]
    n_past = k_past.shape[1]
    s_past = jnp.einsum('bqhmd,bkhmd->bhmqk', q, k_past).astype(jnp.float32) * ATT_SCALE
    s_new = jnp.einsum('bqhmd,bkhmd->bhmqk', q, k).astype(jnp.float32) * ATT_SCALE
    causal = jnp.tril(jnp.ones((t, t), dtype=bool))
    s_new = jnp.where(causal, s_new, -jnp.inf)
    p = jax.nn.softmax(jnp.concatenate([s_past, s_new], axis=-1), axis=-1)
    a = (p[:, :, 0] - lam * p[:, :, 1]).astype(v.dtype)
    return (jnp.einsum('bhqk,bkhd->bqhd', a[..., :n_past], v_past)
            + jnp.einsum('bhqk,bkhd->bqhd', a[..., n_past:], v))


def short_conv(cin, buf, w):
    t = cin.shape[1]
    up = jnp.concatenate([buf, cin], axis=1)
    y = up[:, 0:t] * w[0]
    for j in range(1, CONV_WIDTH):
        y = y + up[:, j:j + t] * w[j]
    return y, up[:, -(CONV_WIDTH - 1):]


def chunk_mix(vn, w_s_l, b_s_l, L):
    b_, t = vn.shape[0], vn.shape[1]
    vr = vn.reshape(b_, t // L, L, N_CHUNK_GROUPS, CHUNK_GROUP_DIM)
    w_m = (w_s_l * jnp.tril(jnp.ones((CHUNK, CHUNK), w_s_l.dtype)))[:, :L, :L]
    out = jnp.einsum('gts,bnsgc->bntgc', w_m, vr) + b_s_l[:, :L].T[:, :, None]
    return out.reshape(b_, t, W_CHUNK)


def trunk_layer(x, c, l, conv_buf, chunk_len, attend, w_ada_l, b_ada_l, w_in_l, lam_l,
                subln_g_l, conv_w_l, chunk_ln_g_l, chunk_ln_b_l, chunk_w_s_l, chunk_b_s_l,
                w_branch_l, w_out_l, ln_g_l, ln_b_l):
    b_, t = x.shape[0], x.shape[1]
    mod = jax.nn.silu(c) @ w_ada_l + b_ada_l
    shift, scale, gate = jnp.split(mod, 3, axis=-1)
    h = x * (1.0 + scale[:, None]) + shift[:, None]
    z = h @ w_in_l
    (q, k, v, g_att, cb, cc, cx, g_conv, cu, cv, g_chunk, merge) = jnp.split(z, COL_SPLITS, axis=-1)
    q = q.reshape(b_, t, N_ATT_HEADS, 2, QK_DIM)
    k = k.reshape(b_, t, N_ATT_HEADS, 2, QK_DIM)
    v = v.reshape(b_, t, N_ATT_HEADS, V_DIM)
    lam = diff_lambda(lam_l, l)
    o = attend(q, k, v, lam)
    o = rms_norm(o, subln_g_l) * (1.0 - lambda_init(l))
    y_att = o.reshape(b_, t, W_ATT) * jax.nn.silu(g_att)
    conv, conv_tail = short_conv(cc * cx, conv_buf, conv_w_l)
    y_conv = cb * conv * jax.nn.silu(g_conv)
    vn = layer_norm(cv, chunk_ln_g_l, chunk_ln_b_l)
    y_chunk = cu * chunk_mix(vn, chunk_w_s_l, chunk_b_s_l, chunk_len) * jax.nn.silu(g_chunk)
    branches = jnp.stack([y_att, y_conv, y_chunk], axis=2)
    proj = jnp.einsum('btnw,nwd->btnd', branches, w_branch_l)
    gates = jax.nn.sigmoid(merge.reshape(b_, t, N_BRANCH, D_MODEL))
    sub = jnp.sum(gates * proj, axis=2) @ w_out_l
    y = layer_norm(ALPHA * x + gate[:, None] * sub, ln_g_l, ln_b_l)
    return y, k, v, conv_tail, vn


def setup_inputs(seed: int = 0) -> dict:
    key = jax.random.key(seed)
    ks = jax.random.split(key, 24)
    n_pages = PAST_LEN // PAGE_SIZE
    n_pool = (5 * DEC_BATCH * n_pages) // 4
    nrm = jax.random.normal
    f32 = jnp.float32
    perm = jax.random.permutation(ks[7], n_pool)[: DEC_BATCH * n_pages]
    return {
        "x_prompt": nrm(ks[0], (BATCH, SEQ, D_MODEL), f32),
        "x_sample": nrm(ks[1], (DEC_BATCH, DEC_SEQ, D_MODEL), f32),
        "c_prompt": nrm(ks[2], (BATCH, D_MODEL), f32),
        "c_sample": nrm(ks[3], (DEC_BATCH, D_MODEL), f32),
        "cache_k": nrm(ks[4], (DEPTH, n_pool, PAGE_SIZE, N_ATT_HEADS, 2, QK_DIM), f32),
        "cache_v": nrm(ks[5], (DEPTH, n_pool, PAGE_SIZE, N_ATT_HEADS, V_DIM), f32),
        "state_conv": nrm(ks[6], (DEPTH, DEC_BATCH, CONV_WIDTH - 1, W_CONV), f32),
        "page_table": perm.reshape(DEC_BATCH, n_pages).astype(jnp.int32),
        "w_ada": nrm(ks[8], (DEPTH, D_MODEL, 3 * D_MODEL), f32) * D_MODEL ** -0.5,
        "b_ada": 0.01 * nrm(ks[9], (DEPTH, 3 * D_MODEL), f32),
        "w_in": nrm(ks[10], (DEPTH, D_MODEL, D_IN), f32) * D_MODEL ** -0.5,
        "diff_lambda": 0.1 * nrm(ks[11], (DEPTH, 4, QK_DIM), f32),
        "subln_g": 1.0 + 0.01 * nrm(ks[12], (DEPTH, V_DIM), f32),
        "conv_w": nrm(ks[13], (DEPTH, CONV_WIDTH, W_CONV), f32) * CONV_WIDTH ** -0.5,
        "chunk_ln_g": 1.0 + 0.01 * nrm(ks[14], (DEPTH, W_CHUNK), f32),
        "chunk_ln_b": 0.01 * nrm(ks[15], (DEPTH, W_CHUNK), f32),
        "chunk_w_s": nrm(ks[16], (DEPTH, N_CHUNK_GROUPS, CHUNK, CHUNK), f32) * CHUNK ** -0.5,
        "chunk_b_s": 1.0 + 0.01 * nrm(ks[17], (DEPTH, N_CHUNK_GROUPS, CHUNK), f32),
        "w_branch": nrm(ks[18], (DEPTH, N_BRANCH, BRANCH_W, D_MODEL), f32) * (BRANCH_W ** -0.5 * BETA),
        "w_out": nrm(ks[19], (DEPTH, D_MODEL, D_MODEL), f32) * (D_MODEL ** -0.5 * BETA),
        "ln_g": 1.0 + 0.01 * nrm(ks[20], (DEPTH, D_MODEL), f32),
        "ln_b": 0.01 * nrm(ks[21], (DEPTH, D_MODEL), f32),
    }


def reference(x_prompt, x_sample, c_prompt, c_sample, cache_k, cache_v, state_conv, page_table,
              w_ada, b_ada, w_in, diff_lambda, subln_g, conv_w, chunk_ln_g, chunk_ln_b,
              chunk_w_s, chunk_b_s, w_branch, w_out, ln_g, ln_b):
    dec_b = x_sample.shape[0]
    dec_t = x_sample.shape[1]
    past = page_table.shape[1] * PAGE_SIZE
    xp, xs = x_prompt, x_sample
    kp_l, vp_l, cp_l, ks_l, vs_l, cs_l, gs_l = [], [], [], [], [], [], []
    for l in range(DEPTH):
        w_l = (w_ada[l], b_ada[l], w_in[l], diff_lambda[l], subln_g[l], conv_w[l],
               chunk_ln_g[l], chunk_ln_b[l], chunk_w_s[l], chunk_b_s[l],
               w_branch[l], w_out[l], ln_g[l], ln_b[l])
        conv0 = jnp.zeros((xp.shape[0], CONV_WIDTH - 1, W_CONV), xp.dtype)
        xp, kp, vp, cp, _ = trunk_layer(xp, c_prompt, l, conv0, CHUNK, diff_attn_prompt, *w_l)
        k_past = cache_k[l][page_table].reshape(dec_b, past, N_ATT_HEADS, 2, QK_DIM)
        v_past = cache_v[l][page_table].reshape(dec_b, past, N_ATT_HEADS, V_DIM)
        attend = functools.partial(diff_attn_sample, k_past=k_past, v_past=v_past)
        xs, kss, vss, css, gss = trunk_layer(xs, c_sample, l, state_conv[l], dec_t, attend, *w_l)
        kp_l.append(kp); vp_l.append(vp); cp_l.append(cp)
        ks_l.append(kss); vs_l.append(vss); cs_l.append(css); gs_l.append(gss)
    return (xp, xs, jnp.stack(kp_l), jnp.stack(vp_l), jnp.stack(cp_l),
            jnp.stack(ks_l), jnp.stack(vs_l), jnp.stack(cs_l), jnp.stack(gs_l))
```

```python
import math
from contextlib import ExitStack

import numpy as np
import ml_dtypes
import concourse.bass as bass
import concourse.mybir as mybir
from concourse.bass_utils import run_bass_kernel_spmd

F32 = mybir.dt.float32
BF16 = mybir.dt.bfloat16
I32 = mybir.dt.int32
AF = mybir.ActivationFunctionType
ALU = mybir.AluOpType
AX = mybir.AxisListType

D = 1024
DEPTH = 2
NS = 32
NPAGES = 64
ROWS = 16
UR = 2
NU = ROWS // UR
NT = 2048
D_IN = 8704
ALPHA = (2.0 * DEPTH) ** 0.25
LN_EPS = 1e-5
ATT_SCALE = 64 ** -0.5


def lambda_init(l):
    return 0.8 - 0.6 * math.exp(-0.3 * l)


class Buf:
    __slots__ = ("name", "w", "rd", "const", "strict")

    def __init__(self, name, const=False, strict=False):
        self.name = name
        self.w = None
        self.rd = {}
        self.const = const
        self.strict = strict


class Sched:
    ENGS = ("pe", "act", "dve", "pool", "sp")

    def __init__(self, nc, es):
        self.nc = nc
        self.es = es
        self.ops = {k: [] for k in self.ENGS}
        self.esem = {k: es.enter_context(nc.semaphore("s_" + k)) for k in ("pe", "act", "dve", "pool")}
        self.cnt = {k: 0 for k in self.esem}
        self.waited = {k: {} for k in self.ENGS}
        self.nsem = 0
        self.dsems = []

    def newsem(self, name):
        self.nsem += 1
        ds = [self.es.enter_context(self.nc.semaphore(name)), 0]
        self.dsems.append(ds)
        return ds

    def barrier(self):
        for eng in self.ENGS:
            waits = []
            for k in ("pe", "act", "dve", "pool"):
                if k != eng and self.cnt[k] > 0:
                    waits.append((self.esem[k], self.cnt[k]))
            for ds in self.dsems:
                if ds[1] > 0:
                    waits.append((ds[0], ds[1]))
            for sem, val in waits:
                self.waited[eng][id(sem)] = max(self.waited[eng].get(id(sem), 0), val)
            self.ops[eng].append((waits, None, None, 0))

    def _waits(self, eng, reads, writes):
        toks = []
        for b in reads:
            if b.w is not None:
                toks.append(b.w + (b.strict,))
        for b in writes:
            if b.w is not None:
                toks.append(b.w + (b.strict,))
            toks.extend(t + (b.strict,) for t in b.rd.values())
        waits = []
        wd = self.waited[eng]
        for (teng, sem, val, strict) in toks:
            if teng == eng and not (strict and eng != "pe"):
                continue
            k = id(sem)
            if wd.get(k, 0) >= val:
                continue
            wd[k] = val
            waits.append((sem, val))
        return waits

    def _mark(self, tok, key, reads, writes):
        for b in reads:
            if not b.const:
                b.rd[key] = tok
        for b in writes:
            b.w = tok
            b.rd = {}

    def op(self, eng, fn, r=(), w=()):
        waits = self._waits(eng, r, w)
        self.cnt[eng] += 1
        sem = self.esem[eng]
        tok = (eng, sem, self.cnt[eng])
        self.ops[eng].append((waits, fn, sem, 1))
        self._mark(tok, eng, r, w)
        return tok

    def dma(self, q, fn, dsem, r=(), w=()):
        waits = self._waits(q, r, w)
        dsem[1] += 16
        tok = (None, dsem[0], dsem[1])
        self.ops[q].append((waits, fn, dsem[0], 16))
        self._mark(tok, id(dsem[0]), r, w)
        return tok

    def cc(self, fn, csem, r=(), w=()):
        waits = self._waits("pool", r, w)
        csem[1] += 1
        tok = (None, csem[0], csem[1])
        self.ops["pool"].append((waits, fn, csem[0], 1))
        self._mark(tok, id(csem[0]), r, w)
        return tok

    def final_wait(self, q, sems):
        for ds in sems:
            if ds[1] > 0:
                self.ops[q].append(([(ds[0], ds[1])], None, None, 0))

    def emit(self, block):
        def mk(key):
            def f(eng):
                for waits, fn, sem, inc in self.ops[key]:
                    for ws, wv in waits:
                        eng.wait_ge(ws, wv)
                    if fn is not None:
                        fn(eng).then_inc(sem, inc)
            return f
        block.tensor(mk("pe"))
        block.scalar(mk("act"))
        block.vector(mk("dve"))
        block.gpsimd(mk("pool"))
        block.sync(mk("sp"))


class Ctx:
    pass


class _Stop(Exception):
    pass


def build_nc(n_pool=2560, do_prompt=True, do_sample=True, use_cc=True, skip_attn=False, debug=False, plevel=99):
    nc = bass.Bass("TRN2", target_bir_lowering=False)
    es = ExitStack()
    S = Sched(nc, es)
    c = Ctx()

    def din(name, shape, dt=F32):
        return nc.dram_tensor(name, list(shape), dt, kind="ExternalInput").ap()

    def dout(name, shape, dt=F32):
        return nc.dram_tensor(name, list(shape), dt, kind="ExternalOutput").ap()

    def dint(name, shape, dt=F32):
        return nc.dram_tensor(name, list(shape), dt, kind="Internal").ap()

    xs_d = din("xs", [NS, D]); cs_d = din("cs", [NS, D])
    pt_d = din("pt", [NS, NPAGES], I32)
    ck_d = [[din(f"ck{l}_{u}", [n_pool, UR * 512]) for u in range(NU)] for l in range(DEPTH)]
    cv_d = [[din(f"cv{l}_{u}", [n_pool, UR * 512]) for u in range(NU)] for l in range(DEPTH)]
    stc_d = din("stc", [DEPTH, NS, 2, 512])
    w_ada_d = din("w_ada", [DEPTH, D, 3 * D]); b_ada_d = din("b_ada", [DEPTH, 3 * D])
    w_in_d = din("w_in", [DEPTH, D, D_IN])
    dlam_d = din("dlam", [DEPTH, 256]); subg_d = din("subg", [DEPTH, 128])
    convw_d = din("convw", [DEPTH, 3, 512])
    clng_d = din("clng", [DEPTH, 512]); clnb_d = din("clnb", [DEPTH, 512])
    cws_d = din("cws", [DEPTH, 4, 128, 128]); cbs_d = din("cbs", [DEPTH, 4, 128])
    wbr_d = din("wbr", [DEPTH, 3, 512, D]); wout_d = din("wout", [DEPTH, D, D])
    lng_d = din("lng", [DEPTH, D]); lnb_d = din("lnb", [DEPTH, D])
    xp_d = din("xp", [NT, D]); cp_d = din("cp", [1, D]); mb_d = din("mb", [128, 1])
    hf_d = din("hf", [128, 1]); xprev_d = din("xprev", [2, D])
    identb_d = din("identb", [128, 128], BF16); identf_d = din("identf", [128, 128])
    trilf_d = din("trilf", [128, 128]); sel_d = din("sel", [NS, 16, 128])

    ys_o = dout("ys", [NS, D]); ks_o = dout("ks", [DEPTH, NS, 512]); vs_o = dout("vs", [DEPTH, NS, 512])
    convs_o = dout("convs", [DEPTH, NS, 2, 512]); cvs_o = dout("cvs", [DEPTH, NS, 512])
    yp_o = dout("yp", [NT, D]); kp_o = dout("kp", [DEPTH, NT, 512]); vp_o = dout("vp", [DEPTH, NT, 512])
    convp_o = dout("convp", [DEPTH, 2, 512])

    part_d = [dint(f"part{l}", [128, 512]) for l in range(DEPTH)]
    part_v = [p.rearrange("(a b) c -> a (b c)", b=2) for p in part_d]
    g4_d = [dint(f"g4_{l}", [4 * 128, 512]) for l in range(DEPTH)]
    g4b_d = [dint(f"g4b_{l}", [4 * 128, 512]) for l in range(DEPTH)]
    g8_d = [dint(f"g8_{l}", [8 * 128, 512]) for l in range(DEPTH)]

    es_s = ExitStack()
    cur = [es]

    def sb(name, shape, dt=F32):
        t = cur[0].enter_context(nc.sbuf_tensor("sb_" + name, list(shape), dt))
        return t

    def psb(name, shape, dt=F32):
        return es.enter_context(nc.psum_tensor(name, list(shape), dt))

    st_sem = S.newsem("st")
    dbg_o = dout("dbg", [12, 128, 1024]) if debug else None
    dbgy_o = dout("dbgy", [128, 12 * 512], BF16) if debug else None
    dbgm_o = dout("dbgm", [128, 8 * 512]) if debug else None
    dbgv_o = dout("dbgv", [2, 128, 4 * 512], BF16) if debug else None
    dbgx_o = dout("dbgx", [2, 128, 4 * 512]) if debug else None

    def dump(slot, ap, buf, rows, cols):
        if debug:
            S.dma("sp", lambda e: e.dma_start(out=dbg_o[slot, 0:rows, 0:cols], in_=ap), st_sem, r=[buf])

    cst_sem = S.newsem("cst")
    cc_sem = S.newsem("ccs")

    PS = [psb(f"ps{i}", [128, 512], F32) for i in range(7)]
    PSB = [Buf(f"ps{i}") for i in range(7)]
    PT = psb("pst", [128, 512], BF16)
    PTB = Buf("pst")

    consts = []

    def cload(name, shape, src, dt=F32, q="sp", slow=False):
        t = sb(name, shape, dt)
        b = Buf(name, const=True)
        S.dma(q, lambda e, t=t, src=src: e.dma_start(out=t[:], in_=src, allow_slow_non_contiguous=slow), cst_sem, w=[b])
        consts.append(b)
        return t, b

    identb, identb_b = cload("identb", [128, 128], identb_d, BF16)
    identf, identf_b = cload("identf", [128, 128], identf_d)
    trilf, trilf_b = cload("trilf", [128, 128], trilf_d)
    ones_f = sb("ones_f", [128, 128]); ones_b = Buf("ones_f", const=True)
    S.op("pool", lambda e: e.memset(ones_f[:], 1.0), w=[ones_b])
    zeros_f = sb("zeros_f", [128, 16]); zeros_b = Buf("zeros_f", const=True)
    S.op("pool", lambda e: e.memset(zeros_f[:], 0.0), w=[zeros_b])

    NSTG, NWB = 1, 2
    stg = [sb(f"wstg{i}", [128, 4, 512]) for i in range(NSTG)]
    stg_b = [Buf(f"wstg{i}") for i in range(NSTG)]
    stg_sem = [S.newsem(f"wstg_s{i}") for i in range(NSTG)]
    wbf = [sb(f"wbf{i}", [128, 8, 512], BF16) for i in range(NWB)]
    wbf_b = [Buf(f"wbf{i}") for i in range(NWB)]
    wctr = [0, 0]

    def wblock(src2d, kc):
        j = wctr[1] % NWB; wctr[1] += 1
        srcv = src2d.rearrange("(kc p) n -> p kc n", p=128)
        for k0 in range(0, kc, 4):
            i = wctr[0] % NSTG; wctr[0] += 1
            S.dma("sp", lambda e, i=i, srcv=srcv, k0=k0: e.dma_start(out=stg[i][:, :, :], in_=srcv[:, k0:k0 + 4, :]),
                  stg_sem[i], w=[stg_b[i]])
            S.op("pool", lambda e, i=i, j=j, k0=k0: e.tensor_copy(out=wbf[j][:, k0:k0 + 4, :], in_=stg[i][:, :, :]),
                 r=[stg_b[i]], w=[wbf_b[j]])
        return wbf[j], wbf_b[j]

    psrr = [0]

    def next_ps():
        i = psrr[0] % 4
        psrr[0] += 1
        return PS[i], PSB[i]

    lamt = sb("lamt", [128, 256]); lamt_b = Buf("lamt")
    lam2 = sb("lam2", [128, 4]); lam2_b = Buf("lam2", strict=True)
    subg = sb("subg", [128, 128]); subg_b = Buf("subg")
    clng = sb("clng", [128, 512]); clng_b = Buf("clng")
    clnb = sb("clnb", [128, 512]); clnb_b = Buf("clnb")
    ws00 = sb("ws00", [128, 4]); ws00_b = Buf("ws00")
    bs0 = sb("bs0", [128, 4]); bs0_b = Buf("bs0")
    lng = sb("lng", [128, D]); lng_b = Buf("lng")
    lnb = sb("lnb", [128, D]); lnb_b = Buf("lnb")

    mv = sb("mv", [128, 2]); mv_b = Buf("mv", strict=True)
    small_sem = S.newsem("small")
    trilb = sb("trilb", [128, 128], BF16); trilb_b = Buf("trilb", const=True)
    onesb = sb("onesb", [128, 128], BF16); onesb_b = Buf("onesb", const=True)
    zerosb = sb("zerosb", [128, 128], BF16); zerosb_b = Buf("zerosb", const=True)
    S.op("pool", lambda e: e.memset(onesb[:], 1.0), w=[onesb_b])
    S.op("pool", lambda e: e.memset(zerosb[:], 0.0), w=[zerosb_b])

    cur[0] = es_s
    sel, sel_b = cload("sel", [NS, 16, 128], sel_d)
    pidx, pidx_b = cload("pidx", [128, 16], pt_d.rearrange("(pr two) j -> (two j) pr", two=2), I32, slow=True)
    convw = sb("convw", [128, 3, 512]); convw_b = Buf("convw")
    xs_t = sb("xs_t", [NS, D]); xs_b = Buf("xs_t")
    S.dma("sp", lambda e: e.dma_start(out=xs_t[:], in_=xs_d), cst_sem, w=[xs_b])
    cs_t = sb("cs_t", [NS, D]); cs_b = Buf("cs_t")
    S.dma("sp", lambda e: e.dma_start(out=cs_t[:], in_=cs_d), cst_sem, w=[cs_b])
    tot = cst_sem[1]
    for b in consts + [xs_b, cs_b]:
        b.w = (None, cst_sem[0], tot)

    scb = sb("scb", [NS, D], BF16); scb_b = Buf("scb")
    S.op("act", lambda e: e.activation(out=scb[:], in_=cs_t[:], func=AF.Silu), r=[cs_b], w=[scb_b])
    scT = sb("scT", [128, 8, NS], BF16); scT_b = Buf("scT")

    def transpose_rows(src_t, src_b, dst_t, dst_b, nchunks, rows):
        for k in range(nchunks):
            S.op("pe", lambda e, k=k: e.transpose(out=PT[:, k * rows:(k + 1) * rows],
                                                  in_=src_t[0:rows, k * 128:(k + 1) * 128],
                                                  identity=identb[0:rows, 0:rows]),
                 r=[src_b, identb_b], w=[PTB])
        S.op("dve", lambda e: e.tensor_copy(out=dst_t[:, 0:nchunks, :],
                                            in_=PT[:, 0:nchunks * rows].rearrange("p (k r) -> p k r", r=rows)),
             r=[PTB], w=[dst_b])

    transpose_rows(scb, scb_b, scT, scT_b, 8, NS)

    mod_s = sb("mod_s", [NS, 3 * D]); mod_b = Buf("mod_s")
    hsb = sb("hsb", [NS, D], BF16); hsb_b = Buf("hsb")
    hsT = sb("hsT", [128, 8, NS], BF16); hsT_b = Buf("hsT")
    zs = sb("zs", [NS, D_IN]); zs_b = Buf("zs")
    sg = sb("sg", [NS, 3, 512]); sg_b = Buf("sg")

    kt = [sb(f"kt{i}", [128, UR * 512]) for i in range(2)]
    kt_b = [Buf(f"kt{i}") for i in range(2)]
    kt_sem = [S.newsem(f"kts{i}") for i in range(2)]
    vt = [sb(f"vt{i}", [128, UR * 512]) for i in range(2)]
    vt_b = [Buf(f"vt{i}") for i in range(2)]
    vt_sem = [S.newsem(f"vts{i}") for i in range(2)]
    prod = sb("prod", [128, UR * 512]); prod_b = Buf("prod")
    qbc = sb("qbc", [128, 512]); qbc_b = Buf("qbc")
    sc_t = sb("sc_t", [128, UR * 8]); sc_b = Buf("sc_t")
    pbd = [sb(f"pbd{i}", [128, UR, 16]) for i in range(2)]
    pbd_b = [Buf(f"pbd{i}") for i in range(2)]
    for i in range(2):
        S.op("pool", lambda e, i=i: e.memset(pbd[i][:], 0.0), w=[pbd_b[i]])
    partt = [sb(f"partt{i}", [4, 1024]) for i in range(1)]
    partt_b = [Buf(f"partt{i}") for i in range(1)]
    part_sem = S.newsem("part")
    G = sb("G", [NS, 2, 2, 520]); G_b = Buf("G")
    g_sem = S.newsem("gsem")
    acc = sb("acc", [NS, 2, 512]); acc_b = Buf("acc")
    lsum = sb("lsum", [NS, 2, 4]); lsum_b = Buf("lsum", strict=True)
    tmpA = sb("tmpA", [NS, 1024]); tmpA_b = Buf("tmpA")
    tmpB = sb("tmpB", [NS, 1024]); tmpB_b = Buf("tmpB")
    sm = sb("sm", [NS, 64]); sm_b = Buf("sm", strict=True)
    att = sb("att", [NS, 512]); att_b = Buf("att")
    Y = sb("Y", [NS, 3, 512]); Y_b = Buf("Y")
    Yb = sb("Yb", [NS, 3 * 512], BF16); Yb_b = Buf("Yb")
    YT = sb("YT", [128, 12, NS], BF16); YT_b = Buf("YT")
    stc_t = sb("stc_t", [NS, 2, 512]); stc_b = Buf("stc_t")
    convo = sb("convo", [NS, 2, 512]); convo_b = Buf("convo")
    vn_s = sb("vn_s", [NS, 512]); vn_sb = Buf("vn_s")
    stats = sb("stats", [128, 4, 6]); stats_b = Buf("stats")
    merged = sb("merged", [NS, D]); merged_b = Buf("merged")
    mergb = sb("mergb", [NS, D], BF16); mergb_b = Buf("mergb")
    mT = sb("mT", [128, 8, NS], BF16); mT_b = Buf("mT")
    res = sb("res", [NS, D]); res_b = Buf("res")

    def V(fn, r=(), w=()):
        return S.op("dve", fn, r=r, w=w)

    def A(fn, r=(), w=()):
        return S.op("act", fn, r=r, w=w)

    def layer_norm_rows(x_t, x_b, rows, width, g_t, g_b, b_t, b_b, out_t, out_b, scr_t=None, scr_b=None):
        if scr_t is None:
            scr_t, scr_b = tmpA, tmpA_b
        V(lambda e: e.tensor_reduce(out=mv[0:rows, 0:1], in_=x_t[0:rows, 0:width], axis=AX.X, op=ALU.add),
          r=[x_b], w=[mv_b])
        V(lambda e: e.tensor_scalar(out=mv[0:rows, 0:1], in0=mv[0:rows, 0:1], scalar1=1.0 / width, scalar2=None,
                                    op0=ALU.mult), r=[mv_b], w=[mv_b])
        V(lambda e: e.tensor_scalar(out=out_t[0:rows, 0:width], in0=x_t[0:rows, 0:width],
                                    scalar1=mv[0:rows, 0:1], scalar2=None, op0=ALU.subtract), r=[x_b, mv_b], w=[out_b])
        V(lambda e: e.tensor_tensor(out=scr_t[0:rows, 0:width], in0=out_t[0:rows, 0:width], in1=out_t[0:rows, 0:width],
                                    op=ALU.mult), r=[out_b], w=[scr_b])
        V(lambda e: e.tensor_reduce(out=mv[0:rows, 1:2], in_=scr_t[0:rows, 0:width], axis=AX.X, op=ALU.add),
          r=[scr_b], w=[mv_b])
        V(lambda e: e.tensor_scalar(out=mv[0:rows, 1:2], in0=mv[0:rows, 1:2], scalar1=1.0 / width, scalar2=LN_EPS,
                                    op0=ALU.mult, op1=ALU.add), r=[mv_b], w=[mv_b])
        A(lambda e: e.activation(out=mv[0:rows, 1:2], in_=mv[0:rows, 1:2], func=AF.Sqrt), r=[mv_b], w=[mv_b])
        V(lambda e: e.reciprocal(out=mv[0:rows, 1:2], in_=mv[0:rows, 1:2]), r=[mv_b], w=[mv_b])
        V(lambda e: e.scalar_tensor_tensor(out=out_t[0:rows, 0:width], in0=out_t[0:rows, 0:width], scalar=mv[0:rows, 1:2],
                                           in1=g_t[0:rows, 0:width], op0=ALU.mult, op1=ALU.mult),
          r=[out_b, mv_b, g_b], w=[out_b])
        V(lambda e: e.tensor_tensor(out=out_t[0:rows, 0:width], in0=out_t[0:rows, 0:width],
                                    in1=b_t[0:rows, 0:width], op=ALU.add), r=[out_b, b_b], w=[out_b])

    def load_layer_params(l, sample=True):
        def ld(t, b, src):
            S.dma("sp", lambda e: e.dma_start(out=t, in_=src), small_sem, w=[b])
        ld(lamt[:], lamt_b, dlam_d[l, :].partition_broadcast(128))
        ld(subg[:], subg_b, subg_d[l, :].partition_broadcast(128))
        if sample:
            ld(convw[:].rearrange("p j c -> p (j c)"), convw_b,
               convw_d[l].rearrange("j c -> (j c)").partition_broadcast(128))
        ld(clng[:], clng_b, clng_d[l, :].partition_broadcast(128))
        ld(clnb[:], clnb_b, clnb_d[l, :].partition_broadcast(128))
        ld(ws00[:], ws00_b, cws_d[l, :, 0, 0].partition_broadcast(128))
        ld(bs0[:], bs0_b, cbs_d[l, :, 0].partition_broadcast(128))
        ld(lng[:], lng_b, lng_d[l, :].partition_broadcast(128))
        ld(lnb[:], lnb_b, lnb_d[l, :].partition_broadcast(128))
        if sample:
            ld(mod_s[:], mod_b, b_ada_d[l, :].partition_broadcast(NS))
            ld(stc_t[:], stc_b, stc_d[l])
        tot = small_sem[1]
        for b in (lamt_b, subg_b, clng_b, clnb_b, ws00_b, bs0_b, lng_b, lnb_b) + ((convw_b, mod_b, stc_b) if sample else ()):
            b.w = (None, small_sem[0], tot)
        V(lambda e: e.tensor_tensor(out=lamt[:, 0:64], in0=lamt[:, 0:64], in1=lamt[:, 64:128], op=ALU.mult),
          r=[lamt_b], w=[lamt_b])
        V(lambda e: e.tensor_tensor(out=lamt[:, 128:192], in0=lamt[:, 128:192], in1=lamt[:, 192:256], op=ALU.mult),
          r=[lamt_b], w=[lamt_b])
        V(lambda e: e.tensor_reduce(out=lam2[:, 0:2], in_=lamt[:].rearrange("p (a b c) -> p a (b c)", a=2, b=2)[:, :, 0:64],
                                    axis=AX.X, op=ALU.add), r=[lamt_b], w=[lam2_b])
        A(lambda e: e.activation(out=lam2[:, 0:2], in_=lam2[:, 0:2], func=AF.Exp), r=[lam2_b], w=[lam2_b])
        V(lambda e: e.tensor_tensor(out=lam2[:, 2:3], in0=lam2[:, 0:1], in1=lam2[:, 1:2], op=ALU.subtract),
          r=[lam2_b], w=[lam2_b])
        V(lambda e: e.tensor_scalar(out=lam2[:, 2:3], in0=lam2[:, 2:3], scalar1=lambda_init(l), scalar2=None,
                                    op0=ALU.add), r=[lam2_b], w=[lam2_b])
        V(lambda e: e.tensor_scalar(out=lam2[:, 3:4], in0=lam2[:, 2:3], scalar1=-1.0, scalar2=None,
                                    op0=ALU.mult), r=[lam2_b], w=[lam2_b])
        V(lambda e: e.tensor_scalar(out=subg[:], in0=subg[:], scalar1=1.0 - lambda_init(l), scalar2=None,
                                    op0=ALU.mult), r=[subg_b], w=[subg_b])

    def sample_layer(l):
        load_layer_params(l)
        for blk in range(6):
            wt, wb = wblock(w_ada_d[l, :, blk * 512:(blk + 1) * 512], 8)
            ps, psb_ = next_ps()
            for k in range(8):
                S.op("pe", lambda e, k=k, wt=wt, ps=ps: e.matmul(ps[0:NS, :], lhsT=scT[:, k, :], rhs=wt[:, k, :],
                                                              start=(k == 0), stop=(k == 7)),
                     r=[scT_b, wb], w=[psb_])
            V(lambda e, ps=ps, blk=blk: e.tensor_tensor(out=mod_s[:, blk * 512:(blk + 1) * 512], in0=ps[0:NS, :],
                                                        in1=mod_s[:, blk * 512:(blk + 1) * 512], op=ALU.add),
              r=[psb_, mod_b], w=[mod_b])
        V(lambda e: e.scalar_tensor_tensor(out=res[:], in0=mod_s[:, D:2 * D], scalar=1.0, in1=xs_t[:],
                                           op0=ALU.add, op1=ALU.mult), r=[mod_b, xs_b], w=[res_b])
        V(lambda e: e.tensor_tensor(out=hsb[:], in0=res[:], in1=mod_s[:, 0:D], op=ALU.add),
          r=[res_b, mod_b], w=[hsb_b])
        transpose_rows(hsb, hsb_b, hsT, hsT_b, 8, NS)
        for blk in range(17):
            wt, wb = wblock(w_in_d[l, :, blk * 512:(blk + 1) * 512], 8)
            ps, psb_ = next_ps()
            for k in range(8):
                S.op("pe", lambda e, k=k, wt=wt, ps=ps: e.matmul(ps[0:NS, :], lhsT=hsT[:, k, :], rhs=wt[:, k, :],
                                                              start=(k == 0), stop=(k == 7)),
                     r=[hsT_b, wb], w=[psb_])
            A(lambda e, ps=ps, blk=blk: e.copy(out=zs[:, blk * 512:(blk + 1) * 512], in_=ps[0:NS, :]),
              r=[psb_], w=[zs_b])
        S.dma("sp", lambda e: e.dma_start(out=ks_o[l], in_=zs[:, 512:1024]), st_sem, r=[zs_b])
        S.dma("sp", lambda e: e.dma_start(out=vs_o[l], in_=zs[:, 1024:1536]), st_sem, r=[zs_b])

        u = 0
        for pr in range(16):
            qps, qpsb = PS[4], PSB[4]
            S.op("pe", lambda e, pr=pr: e.matmul(qps[:, :], lhsT=sel[:, pr, :], rhs=zs[:, 0:512], start=True, stop=True),
                 r=[sel_b, zs_b], w=[qpsb])
            A(lambda e: e.copy(out=qbc[:], in_=qps[:, :]), r=[qpsb], w=[qbc_b])
            aps, apsb = PS[5], PSB[5]
            lps, lpsb = PS[6], PSB[6]
            S.op("pe", lambda e: e.matmul(aps[0:4, :], lhsT=zeros_f[:, 0:4], rhs=qbc[:, :], start=True, stop=False),
                 r=[zeros_b, qbc_b], w=[apsb])
            S.op("pe", lambda e: e.matmul(lps[0:4, 0:8], lhsT=zeros_f[:, 0:4], rhs=ones_f[:, 0:8], start=True, stop=False),
                 r=[zeros_b, ones_b], w=[lpsb])
            for half in range(0 if skip_attn else NU):
                i = u % 2
                u += 1
                c0 = half * UR * 512
                S.dma("pool", lambda e, i=i, pr=pr, half=half: e.indirect_dma_start(
                    out=kt[i][:], out_offset=None, in_=ck_d[l][half],
                    in_offset=bass.IndirectOffsetOnAxis(ap=pidx[:, pr:pr + 1], axis=0)),
                    kt_sem[i], r=[pidx_b], w=[kt_b[i]])
                S.dma("pool", lambda e, i=i, pr=pr, half=half: e.indirect_dma_start(
                    out=vt[i][:], out_offset=None, in_=cv_d[l][half],
                    in_offset=bass.IndirectOffsetOnAxis(ap=pidx[:, pr:pr + 1], axis=0)),
                    vt_sem[i], r=[pidx_b], w=[vt_b[i]])
                V(lambda e, i=i: e.tensor_tensor(out=prod[:].rearrange("p (r c) -> p r c", c=512),
                                                 in0=kt[i][:].rearrange("p (r c) -> p r c", c=512),
                                                 in1=qbc[:].unsqueeze(1).to_broadcast([128, UR, 512]), op=ALU.mult),
                  r=[kt_b[i], qbc_b], w=[prod_b])
                V(lambda e: e.tensor_reduce(out=sc_t[:], in_=prod[:].rearrange("p (x d) -> p x d", d=64),
                                            axis=AX.X, op=ALU.add), r=[prod_b], w=[sc_b])
                for s2 in range(2):
                    A(lambda e, i=i, s2=s2: e.activation(
                        out=pbd[i][s2 * 64:(s2 + 1) * 64].rearrange("p r (h s m) -> p r h s m", h=4, s=2)[:, :, :, s2, :],
                        in_=sc_t[s2 * 64:(s2 + 1) * 64, :].rearrange("p (r h m) -> p r h m", h=4, m=2),
                        func=AF.Exp, scale=ATT_SCALE), r=[sc_b], w=[pbd_b[i]])
                if l == 0 and pr == 0 and half == 0:
                    dump(0, kt[i][:, 0:1024], kt_b[i], 128, 1024)
                    dump(1, qbc[:], qbc_b, 128, 512)
                    dump(2, sc_t[:], sc_b, 128, UR * 8)
                    dump(3, pbd[i][:].rearrange("p r c -> p (r c)"), pbd_b[i], 128, UR * 16)
                    dump(4, vt[i][:, 0:1024], vt_b[i], 128, 1024)
                for r_ in range(UR):
                    first = (half == 0 and r_ == 0)
                    last = (half == NU - 1 and r_ == UR - 1)
                    for h in range(4):
                        S.op("pe", lambda e, i=i, r_=r_, h=h, first=first, last=last: e.matmul(
                            aps[0:4, h * 128:(h + 1) * 128], lhsT=pbd[i][:, r_, h * 4:(h + 1) * 4],
                            rhs=vt[i][:, r_ * 512 + h * 128:r_ * 512 + (h + 1) * 128],
                            start=False, stop=last), r=[pbd_b[i], vt_b[i]], w=[apsb])
                        S.op("pe", lambda e, i=i, r_=r_, h=h, first=first, last=last: e.matmul(
                            lps[0:4, h * 2:(h + 1) * 2], lhsT=pbd[i][:, r_, h * 4:(h + 1) * 4], rhs=ones_f[:, 0:2],
                            start=False, stop=last), r=[pbd_b[i], ones_b], w=[lpsb])
            j = 0
            if skip_attn:
                V(lambda e, j=j: e.memset(partt[j][:, 0:520], 1.0), w=[partt_b[j]])
            else:
                V(lambda e, j=j: e.tensor_copy(out=partt[j][:, 0:512], in_=aps[0:4, :]), r=[apsb], w=[partt_b[j]])
                V(lambda e, j=j: e.tensor_copy(out=partt[j][:, 512:520], in_=lps[0:4, 0:8]), r=[lpsb], w=[partt_b[j]])
            S.dma("sp", lambda e, j=j, pr=pr: e.dma_start(out=part_v[l][pr * 4:(pr + 1) * 4, :], in_=partt[j][:, :]),
                  part_sem, r=[partt_b[j]])
            if l == 0 and pr == 0:
                dump(5, partt[j][:, :], partt_b[j], 4, 1024)
        pd_b = Buf("part_d"); pd_b.w = (None, part_sem[0], part_sem[1])
        g8_b = Buf("g8")
        if not use_cc:
            S.dma("sp", lambda e: e.dma_start(out=g8_d[l][0:128, :], in_=part_d[l]), part_sem, r=[pd_b], w=[g8_b])
        if use_cc:
            g4_b = Buf("g4"); g4b_b = Buf("g4b")
            S.cc(lambda e: e.collective_compute("AllGather", ALU.bypass, replica_groups=[[0, 1, 2, 3], [4, 5, 6, 7]],
                                                ins=[part_d[l]], outs=[g4_d[l]]), cc_sem, r=[pd_b], w=[g4_b])
            S.dma("pool", lambda e: e.dma_start(out=g4b_d[l], in_=g4_d[l]), part_sem, r=[g4_b], w=[g4b_b])
            S.cc(lambda e: e.collective_compute("AllGather", ALU.bypass, replica_groups=[[0, 4], [1, 5], [2, 6], [3, 7]],
                                                ins=[g4b_d[l]], outs=[g8_d[l]]), cc_sem, r=[g4b_b], w=[g8_b])
        g8v = g8_d[l].rearrange("(a b) c -> a (b c)", b=2).rearrange("(k s m) c -> s k m c", k=8, m=2)
        for kh in range(4):
            for k2 in range(2):
                for m in range(2):
                    S.dma("sp", lambda e, kh=kh, k2=k2, m=m: e.dma_start(out=G[:, k2, m, :], in_=g8v[:, kh * 2 + k2, m, 0:520]),
                          g_sem, r=[g8_b], w=[G_b])
            G_b.w = (None, g_sem[0], g_sem[1])
            Gl = G[:, :, :, 512:520].rearrange("s k m (h t) -> s m h t k", t=2)[:, :, :, 0, :]
            if kh == 0:
                V(lambda e: e.tensor_reduce(out=acc[:], in_=G[:, :, :, 0:512].rearrange("s k m c -> s m c k"),
                                            axis=AX.X, op=ALU.add), r=[G_b], w=[acc_b])
                V(lambda e, Gl=Gl: e.tensor_reduce(out=lsum[:], in_=Gl, axis=AX.X, op=ALU.add), r=[G_b], w=[lsum_b])
            else:
                V(lambda e: e.tensor_reduce(out=tmpA[:].rearrange("s (m c) -> s m c", m=2),
                                            in_=G[:, :, :, 0:512].rearrange("s k m c -> s m c k"),
                                            axis=AX.X, op=ALU.add), r=[G_b], w=[tmpA_b])
                V(lambda e, Gl=Gl: e.tensor_reduce(out=sm[:, 32:40].rearrange("s (m h) -> s m h", m=2), in_=Gl,
                                                   axis=AX.X, op=ALU.add), r=[G_b], w=[sm_b])
                V(lambda e: e.tensor_tensor(out=acc[:].rearrange("s a c -> s (a c)"), in0=acc[:].rearrange("s a c -> s (a c)"),
                                            in1=tmpA[:], op=ALU.add), r=[acc_b, tmpA_b], w=[acc_b])
                V(lambda e: e.tensor_tensor(out=lsum[:].rearrange("s m h -> s (m h)"), in0=lsum[:].rearrange("s m h -> s (m h)"),
                                            in1=sm[:, 32:40], op=ALU.add), r=[lsum_b, sm_b], w=[lsum_b])
        V(lambda e: e.tensor_tensor(out=tmpA[:, 0:512], in0=zs[:, 0:512], in1=zs[:, 512:1024], op=ALU.mult),
          r=[zs_b], w=[tmpA_b])
        V(lambda e: e.tensor_reduce(out=sm[:, 0:8], in_=tmpA[:, 0:512].rearrange("s (x d) -> s x d", d=64),
                                    axis=AX.X, op=ALU.add), r=[tmpA_b], w=[sm_b])
        A(lambda e: e.activation(out=sm[:, 8:16], in_=sm[:, 0:8], func=AF.Exp, scale=ATT_SCALE), r=[sm_b], w=[sm_b])
        V(lambda e: e.tensor_tensor(out=lsum[:], in0=lsum[:], in1=sm[:, 8:16].rearrange("s (h m) -> s m h", m=2), op=ALU.add),
          r=[lsum_b, sm_b], w=[lsum_b])
        V(lambda e: e.tensor_tensor(
            out=tmpA[:].rearrange("s (m h c) -> s m h c", m=2, h=4),
            in0=zs[:, 1024:1536].rearrange("s (h c) -> s h c", h=4).unsqueeze(1).to_broadcast([NS, 2, 4, 128]),
            in1=sm[:, 8:16].rearrange("s (h m) -> s m h", m=2).unsqueeze(3).to_broadcast([NS, 2, 4, 128]),
            op=ALU.mult), r=[zs_b, sm_b], w=[tmpA_b])
        V(lambda e: e.tensor_tensor(out=acc[:].rearrange("s a c -> s (a c)"), in0=acc[:].rearrange("s a c -> s (a c)"),
                                    in1=tmpA[:], op=ALU.add), r=[acc_b, tmpA_b], w=[acc_b])
        V(lambda e: e.reciprocal(out=sm[:, 16:24], in_=lsum[:].rearrange("s m h -> s (m h)")), r=[lsum_b], w=[sm_b])
        V(lambda e: e.tensor_tensor(out=acc[:].rearrange("s m (h c) -> s (m h) c", h=4),
                                    in0=acc[:].rearrange("s m (h c) -> s (m h) c", h=4),
                                    in1=sm[:, 16:24].unsqueeze(2).to_broadcast([NS, 8, 128]),
                                    op=ALU.mult), r=[acc_b, sm_b], w=[acc_b])
        V(lambda e: e.scalar_tensor_tensor(out=att[:], in0=acc[:, 1, :], scalar=lam2[0:NS, 3:4], in1=acc[:, 0, :],
                                           op0=ALU.mult, op1=ALU.add), r=[acc_b, lam2_b], w=[att_b])
        if l == 0:
            dump(6, att[:], att_b, NS, 512)
            dump(7, acc[:].rearrange("s m c -> s (m c)"), acc_b, NS, 1024)
            dump(8, sm[:], sm_b, NS, 64)
        V(lambda e: e.tensor_tensor(out=tmpB[:, 0:512], in0=att[:], in1=att[:], op=ALU.mult), r=[att_b], w=[tmpB_b])
        V(lambda e: e.tensor_reduce(out=sm[:, 24:28], in_=tmpB[:, 0:512].rearrange("s (h c) -> s h c", h=4),
                                    axis=AX.X, op=ALU.add), r=[tmpB_b], w=[sm_b])
        V(lambda e: e.tensor_scalar(out=sm[:, 24:28], in0=sm[:, 24:28], scalar1=1.0 / 128, scalar2=LN_EPS,
                                    op0=ALU.mult, op1=ALU.add), r=[sm_b], w=[sm_b])
        A(lambda e: e.activation(out=sm[:, 24:28], in_=sm[:, 24:28], func=AF.Sqrt), r=[sm_b], w=[sm_b])
        V(lambda e: e.reciprocal(out=sm[:, 24:28], in_=sm[:, 24:28]), r=[sm_b], w=[sm_b])
        V(lambda e: e.tensor_tensor(out=att[:].rearrange("s (h c) -> s h c", h=4),
                                    in0=att[:].rearrange("s (h c) -> s h c", h=4),
                                    in1=sm[:, 24:28].unsqueeze(2).to_broadcast([NS, 4, 128]), op=ALU.mult),
          r=[att_b, sm_b], w=[att_b])
        V(lambda e: e.tensor_tensor(out=att[:].rearrange("s (h c) -> s h c", h=4),
                                    in0=att[:].rearrange("s (h c) -> s h c", h=4),
                                    in1=subg[0:NS, :].unsqueeze(1).to_broadcast([NS, 4, 128]), op=ALU.mult),
          r=[att_b, subg_b], w=[att_b])
        for gi, blk in enumerate((3, 7, 10)):
            A(lambda e, gi=gi, blk=blk: e.activation(out=sg[:, gi, :], in_=zs[:, blk * 512:(blk + 1) * 512], func=AF.Silu),
              r=[zs_b], w=[sg_b])
        V(lambda e: e.tensor_tensor(out=Y[:, 0, :], in0=att[:], in1=sg[:, 0, :], op=ALU.mult), r=[att_b, sg_b], w=[Y_b])
        zc = lambda blk: zs[:, blk * 512:(blk + 1) * 512]
        V(lambda e: e.tensor_tensor(out=convo[:, 1, :], in0=zc(5), in1=zc(6), op=ALU.mult), r=[zs_b], w=[convo_b])
        V(lambda e: e.tensor_copy(out=convo[:, 0, :], in_=stc_t[:, 1, :]), r=[stc_b], w=[convo_b])
        S.dma("sp", lambda e: e.dma_start(out=convs_o[l], in_=convo[:]), st_sem, r=[convo_b])
        V(lambda e: e.tensor_tensor(out=tmpB[:, 0:512], in0=stc_t[:, 0, :], in1=convw[0:NS, 0, :], op=ALU.mult),
          r=[stc_b, convw_b], w=[tmpB_b])
        V(lambda e: e.tensor_tensor(out=tmpB[:, 512:1024], in0=stc_t[:, 1, :], in1=convw[0:NS, 1, :], op=ALU.mult),
          r=[stc_b, convw_b], w=[tmpB_b])
        V(lambda e: e.tensor_tensor(out=tmpB[:, 0:512], in0=tmpB[:, 0:512], in1=tmpB[:, 512:1024], op=ALU.add),
          r=[tmpB_b], w=[tmpB_b])
        V(lambda e: e.tensor_tensor(out=tmpB[:, 512:1024], in0=convo[:, 1, :], in1=convw[0:NS, 2, :], op=ALU.mult),
          r=[convo_b, convw_b], w=[tmpB_b])
        V(lambda e: e.tensor_tensor(out=tmpB[:, 0:512], in0=tmpB[:, 0:512], in1=tmpB[:, 512:1024], op=ALU.add),
          r=[tmpB_b], w=[tmpB_b])
        V(lambda e: e.tensor_tensor(out=tmpB[:, 0:512], in0=tmpB[:, 0:512], in1=zc(4), op=ALU.mult),
          r=[tmpB_b, zs_b], w=[tmpB_b])
        V(lambda e: e.tensor_tensor(out=Y[:, 1, :], in0=tmpB[:, 0:512], in1=sg[:, 1, :], op=ALU.mult),
          r=[tmpB_b, sg_b], w=[Y_b])
        cvb = Buf("cv_view"); cvb.w = zs_b.w
        layer_norm_rows(zs[:, 9 * 512:10 * 512], zs_b, NS, 512, clng, clng_b, clnb, clnb_b, vn_s, vn_sb)
        S.dma("sp", lambda e: e.dma_start(out=cvs_o[l], in_=vn_s[:]), st_sem, r=[vn_sb])
        for g in range(4):
            V(lambda e, g=g: e.tensor_scalar(out=tmpB[:, g * 128:(g + 1) * 128], in0=vn_s[:, g * 128:(g + 1) * 128],
                                             scalar1=ws00[0:NS, g:g + 1], scalar2=bs0[0:NS, g:g + 1],
                                             op0=ALU.mult, op1=ALU.add), r=[vn_sb, ws00_b, bs0_b], w=[tmpB_b])
        V(lambda e: e.tensor_tensor(out=tmpB[:, 0:512], in0=tmpB[:, 0:512], in1=zc(8), op=ALU.mult),
          r=[tmpB_b, zs_b], w=[tmpB_b])
        V(lambda e: e.tensor_tensor(out=Y[:, 2, :], in0=tmpB[:, 0:512], in1=sg[:, 2, :], op=ALU.mult),
          r=[tmpB_b, sg_b], w=[Y_b])
        V(lambda e: e.tensor_copy(out=Yb[:], in_=Y[:].rearrange("s n c -> s (n c)")), r=[Y_b], w=[Yb_b])
        transpose_rows(Yb, Yb_b, YT, YT_b, 8, NS)
        for k in range(8, 12):
            S.op("pe", lambda e, k=k: e.transpose(out=PT[:, (k - 8) * NS:(k - 7) * NS],
                                                  in_=Yb[0:NS, k * 128:(k + 1) * 128], identity=identb[0:NS, 0:NS]),
                 r=[Yb_b, identb_b], w=[PTB])
        V(lambda e: e.tensor_copy(out=YT[:, 8:12, :], in_=PT[:, 0:4 * NS].rearrange("p (k r) -> p k r", r=NS)),
          r=[PTB], w=[YT_b])
        for n in range(3):
            A(lambda e, n=n: e.activation(out=zs[:, 5632 + n * D:5632 + (n + 1) * D], in_=zs[:, 5632 + n * D:5632 + (n + 1) * D],
                                          func=AF.Sigmoid), r=[zs_b], w=[zs_b])
        for n in range(3):
            for cb in range(2):
                wt, wb = wblock(wbr_d[l, n, :, cb * 512:(cb + 1) * 512], 4)
                ps, psb_ = next_ps()
                for k in range(4):
                    S.op("pe", lambda e, k=k, n=n, wt=wt, ps=ps: e.matmul(ps[0:NS, :], lhsT=YT[:, n * 4 + k, :], rhs=wt[:, k, :],
                                                                       start=(k == 0), stop=(k == 3)),
                         r=[YT_b, wb], w=[psb_])
                dst = merged[:, cb * 512:(cb + 1) * 512]
                gsl = zs[:, 5632 + n * D + cb * 512:5632 + n * D + (cb + 1) * 512]
                if n == 0:
                    V(lambda e, ps=ps, dst=dst, gsl=gsl: e.tensor_tensor(out=dst, in0=ps[0:NS, :], in1=gsl, op=ALU.mult),
                      r=[psb_, zs_b], w=[merged_b])
                else:
                    V(lambda e, ps=ps, gsl=gsl, cb=cb: e.tensor_tensor(out=tmpB[:, cb * 512:(cb + 1) * 512], in0=ps[0:NS, :], in1=gsl, op=ALU.mult),
                      r=[psb_, zs_b], w=[tmpB_b])
                    V(lambda e, dst=dst, cb=cb: e.tensor_tensor(out=dst, in0=dst, in1=tmpB[:, cb * 512:(cb + 1) * 512], op=ALU.add),
                      r=[tmpB_b, merged_b], w=[merged_b])
        V(lambda e: e.tensor_copy(out=mergb[:], in_=merged[:]), r=[merged_b], w=[mergb_b])
        transpose_rows(mergb, mergb_b, mT, mT_b, 8, NS)
        for cb in range(2):
            wt, wb = wblock(wout_d[l, :, cb * 512:(cb + 1) * 512], 8)
            ps, psb_ = next_ps()
            for k in range(8):
                S.op("pe", lambda e, k=k, wt=wt, ps=ps: e.matmul(ps[0:NS, :], lhsT=mT[:, k, :], rhs=wt[:, k, :],
                                                              start=(k == 0), stop=(k == 7)),
                     r=[mT_b, wb], w=[psb_])
            V(lambda e, ps=ps, cb=cb: e.tensor_tensor(out=res[:, cb * 512:(cb + 1) * 512], in0=ps[0:NS, :],
                                                      in1=mod_s[:, 2 * D + cb * 512:2 * D + (cb + 1) * 512], op=ALU.mult),
              r=[psb_, mod_b], w=[res_b])
        V(lambda e: e.scalar_tensor_tensor(out=res[:], in0=xs_t[:], scalar=ALPHA, in1=res[:], op0=ALU.mult, op1=ALU.add),
          r=[xs_b, res_b], w=[res_b])
        layer_norm_rows(res, res_b, NS, D, lng, lng_b, lnb, lnb_b, xs_t, xs_b)

    if do_sample:
        for l in range(DEPTH):
            sample_layer(l)
        S.dma("sp", lambda e: e.dma_start(out=ys_o, in_=xs_t[:]), st_sem, r=[xs_b])
    S.barrier()
    es_s.close()
    cur[0] = es

    def lvl(n):
        if plevel <= n:
            raise _Stop()

    if do_prompt and plevel > -1:
        S.op("pool", lambda e: e.tensor_copy(out=trilb[:], in_=trilf[:]), r=[trilf_b], w=[trilb_b])
        x1_d = dint("x1", [NT, D])
        kx_d = [dint(f"kx{l}", [128, 8192], BF16) for l in range(DEPTH)]
        vx_d = [dint(f"vx{l}", [128, 8192], BF16) for l in range(DEPTH)]
        gk_d = [dint(f"gk{l}", [256, 8192], BF16) for l in range(DEPTH)]
        gv_d = [dint(f"gv{l}", [256, 8192], BF16) for l in range(DEPTH)]
        xtl_d = dint("xtl", [32, 64]); gxt_d = dint("gxt", [64, 64])
        PAIRS = [[0, 1], [2, 3], [4, 5], [6, 7]]

        KTo = sb("KTo", [128, 4, NT], BF16); KTo_b = Buf("KTo")
        KTx = sb("KTx", [128, 4, NT], BF16); KTx_b = Buf("KTx")
        Vo = sb("Vo", [128, 16, 512], BF16); Vo_b = Buf("Vo")
        Vx = sb("Vx", [128, 16, 512], BF16); Vx_b = Buf("Vx")
        modp = sb("modp", [128, 3 * D]); modp_b = Buf("modp")
        cpc = sb("cpc", [128, 8]); cpc_b = Buf("cpc")
        cT_rep = sb("cT_rep", [128, 8, 128], BF16); cT_b = Buf("cT_rep")
        xt = sb("xt", [128, D]); xt_b = Buf("xt"); xt_sem = S.newsem("xt_s")
        ht = sb("ht", [128, D]); ht_b = Buf("ht")
        hb = sb("hb", [128, D], BF16); hb_b = Buf("hb")
        hT = sb("hT", [128, 8, 512], BF16); hT_b = Buf("hT")
        qT = sb("qT", [128, 4, 512], BF16); qT_b = Buf("qT")
        Pt = [sb(f"Pt{i}", [128, 512], BF16) for i in range(2)]; Pt_b = [Buf(f"Pt{i}") for i in range(2)]
        sqb = sb("sqb", [128, 512], BF16); sqb_b = Buf("sqb")
        osb = [sb(f"osb{i}", [128, 512]) for i in range(3)]; osb_b = [Buf(f"osb{i}") for i in range(3)]
        yT = sb("yT", [128, 12, 512], BF16); yT_b = Buf("yT")
        WK = sb("WK", [128, 8, 512]); WK_b = [Buf(f"WK{i}") for i in range(8)]
        mrgT = sb("mrgT", [128, 8, 512], BF16); mrgT_b = Buf("mrgT")
        vn = sb("vn", [128, 4, 512], BF16); vn_b = Buf("vn")
        cvt = sb("cvt", [128, 512]); cvt_b = Buf("cvt")
        vnf = sb("vnf", [128, 512]); vnf_b = Buf("vnf")
        wsf = sb("wsf", [128, 128]); wsf_b = Buf("wsf")
        wsfb = sb("wsfb", [128, 128], BF16); wsfb_b = Buf("wsfb")
        wsT = sb("wsT", [128, 4, 128], BF16); wsT_b = Buf("wsT")
        bsb = sb("bsb", [128, 4, 128]); bsb_b = Buf("bsb")
        kst, kst_b = cvt, cvt_b; kst_sem = S.newsem("kst_s")
        vst, vst_b = vnf, vnf_b; vst_sem = S.newsem("vst_s")
        rr = sb("rr", [128, D]); rr_b = Buf("rr")
        yo, yo_b = ht, ht_b; yo_sem = S.newsem("yo_s")
        cwc = sb("cwc", [128, 4, 3]); cwc_b = Buf("cwc", strict=True)
        gsc = sb("gsc", [128, 1]); gsc_b = Buf("gsc", strict=True)
        ucar = sb("ucar", [128, 4, 2]); ucar_b = Buf("ucar", strict=True)
        mbt = sb("mbt", [128, 1]); mbt_b = Buf("mbt")
        hft = sb("hft", [128, 1]); hft_b = Buf("hft")
        xpv = sb("xpv", [2, D]); xpv_b = Buf("xpv")
        hpb = sb("hpb", [2, D], BF16); hpb_b = Buf("hpb")
        hTp = sb("hTp", [128, 8, 2], BF16); hTp_b = Buf("hTp")
        tmp2 = sb("tmp2", [128, 2]); tmp2_b = Buf("tmp2")
        pp_sem = S.newsem("pp_s")
        xch_sem = S.newsem("xch_s")
        x1_sem = S.newsem("x1_s")
        x1_b = Buf("x1")

        def ldp(t, b, src):
            S.dma("sp", lambda e: e.dma_start(out=t, in_=src), pp_sem, w=[b])

        p3 = [0]

        def next3():
            i = p3[0] % 3
            p3[0] += 1
            return PS[i], PSB[i]

        def fm_proj(wt, wb, nk, col0, rhs_of, rbufs, ncols=512):
            ps, pb = next3()
            for k in range(nk):
                S.op("pe", lambda e, k=k, ps=ps: e.matmul(ps[:, 0:ncols], lhsT=wt[:, k, col0:col0 + 128], rhs=rhs_of(k),
                                                        start=(k == 0), stop=(k == nk - 1)), r=[wb] + rbufs, w=[pb])
            return ps, pb

        ldp(mbt[:], mbt_b, mb_d)
        ldp(hft[:], hft_b, hf_d)
        ldp(cpc[:], cpc_b, cp_d.rearrange("o (k p) -> p (o k)", p=128))
        for b in (mbt_b, hft_b, cpc_b):
            b.w = (None, pp_sem[0], pp_sem[1])
        A(lambda e: e.activation(out=cpc[:], in_=cpc[:], func=AF.Silu), r=[cpc_b], w=[cpc_b])
        V(lambda e: e.tensor_copy(out=cT_rep[:], in_=cpc[:].unsqueeze(2).to_broadcast([128, 8, 128])), r=[cpc_b], w=[cT_b])

        def prompt_params(l):
            load_layer_params(l, sample=False)
            ldp(modp[:], modp_b, b_ada_d[l, :].partition_broadcast(128))
            for j in range(3):
                ldp(cwc[:, :, j], cwc_b, convw_d[l, j, :].rearrange("(c p) -> p c", p=128))
            ldp(gsc[:], gsc_b, subg_d[l, :].rearrange("(p o) -> p o", o=1))
            ldp(bsb[:].rearrange("p g t -> p (g t)"), bsb_b, cbs_d[l].rearrange("g t -> (g t)").partition_broadcast(128))
            for b in (modp_b, cwc_b, gsc_b, bsb_b):
                b.w = (None, pp_sem[0], pp_sem[1])
            V(lambda e: e.tensor_scalar(out=gsc[:], in0=gsc[:], scalar1=1.0 - lambda_init(l), scalar2=None, op0=ALU.mult),
              r=[gsc_b], w=[gsc_b])
            lvl(-0.5)
            for blk in range(6):
                wt, wb = wblock(w_ada_d[l, :, blk * 512:(blk + 1) * 512], 8)
                ps, pb = next3()
                for k in range(8):
                    S.op("pe", lambda e, k=k, wt=wt, ps=ps: e.matmul(ps[:, :], lhsT=cT_rep[:, k, :], rhs=wt[:, k, :],
                                                                  start=(k == 0), stop=(k == 7)), r=[cT_b, wb], w=[pb])
                V(lambda e, ps=ps, blk=blk: e.tensor_tensor(out=modp[:, blk * 512:(blk + 1) * 512], in0=ps[:, :],
                                                            in1=modp[:, blk * 512:(blk + 1) * 512], op=ALU.add),
                  r=[pb, modp_b], w=[modp_b])
            V(lambda e: e.tensor_scalar(out=modp[:, D:2 * D], in0=modp[:, D:2 * D], scalar1=1.0, scalar2=None, op0=ALU.add),
              r=[modp_b], w=[modp_b])
            lvl(-0.3)
            for g in range(4):
                S.dma("sp", lambda e, g=g: e.dma_start(out=wsf[:], in_=cws_d[l, g]), pp_sem, w=[wsf_b])
                V(lambda e: e.tensor_copy(out=wsfb[:], in_=wsf[:]), r=[wsf_b], w=[wsfb_b])
                S.op("pe", lambda e: e.transpose(out=PT[:, 0:128], in_=wsfb[:], identity=identb[:]),
                     r=[wsfb_b, identb_b], w=[PTB])
                V(lambda e, g=g: e.tensor_tensor(out=wsT[:, g, :], in0=PT[:, 0:128], in1=trilb[:], op=ALU.mult),
                  r=[PTB, trilb_b], w=[wsT_b])

        def compute_hT(st, xsrc, xsrc_b):
            for tt in range(4):
                tok0 = st * 512 + tt * 128
                S.dma("sp", lambda e, tok0=tok0: e.dma_start(out=xt[:], in_=xsrc[tok0:tok0 + 128, :]), xt_sem,
                      r=[xsrc_b], w=[xt_b])
                V(lambda e: e.tensor_tensor(out=ht[:], in0=xt[:], in1=modp[:, D:2 * D], op=ALU.mult),
                  r=[xt_b, modp_b], w=[ht_b])
                V(lambda e: e.tensor_tensor(out=hb[:], in0=ht[:], in1=modp[:, 0:D], op=ALU.add),
                  r=[ht_b, modp_b], w=[hb_b])
                lvl(0.15)
                for k2 in range(2):
                    for kk in range(4):
                        k = k2 * 4 + kk
                        S.op("pe", lambda e, k=k, kk=kk: e.transpose(out=PT[:, kk * 128:(kk + 1) * 128], in_=hb[:, k * 128:(k + 1) * 128],
                                                                  identity=identb[:]), r=[hb_b, identb_b], w=[PTB])
                    V(lambda e, tt=tt, k2=k2: e.tensor_copy(out=hT[:, 4 * k2:4 * k2 + 4, tt * 128:(tt + 1) * 128],
                                                         in_=PT[:, 0:512].rearrange("p (k t) -> p k t", t=128)),
                      r=[PTB], w=[hT_b])
                lvl(0.18)

        def phase_a(l, xsrc, xsrc_b):
            Wk, Wk_b = wblock(w_in_d[l, :, 512:1024], 8)
            Wv, Wv_b = wblock(w_in_d[l, :, 1024:1536], 8)
            lvl(0.1)
            for st in range(4):
                compute_hT(st, xsrc, xsrc_b)
                lvl(0.2)
                for tt in range(4):
                    tok0 = st * 512 + tt * 128
                    ps, pb = next3()
                    for k in range(8):
                        S.op("pe", lambda e, k=k, ps=ps, tt=tt: e.matmul(ps[:, :], lhsT=hT[:, k, tt * 128:(tt + 1) * 128], rhs=Wk[:, k, :],
                                                                       start=(k == 0), stop=(k == 7)), r=[hT_b, Wk_b], w=[pb])
                    A(lambda e, ps=ps: e.copy(out=kst[:], in_=ps[:, :]), r=[pb], w=[kst_b])
                    S.dma("sp", lambda e, tok0=tok0: e.dma_start(out=kp_o[l, tok0:tok0 + 128, :], in_=kst[:]), kst_sem, r=[kst_b])
                    ps2, pb2 = next3()
                    for k in range(8):
                        S.op("pe", lambda e, k=k, ps2=ps2, tt=tt: e.matmul(ps2[:, :], lhsT=hT[:, k, tt * 128:(tt + 1) * 128], rhs=Wv[:, k, :],
                                                                         start=(k == 0), stop=(k == 7)), r=[hT_b, Wv_b], w=[pb2])
                    A(lambda e, ps2=ps2: e.copy(out=vst[:], in_=ps2[:, :]), r=[pb2], w=[vst_b])
                    S.dma("sp", lambda e, tok0=tok0: e.dma_start(out=vp_o[l, tok0:tok0 + 128, :], in_=vst[:]), vst_sem, r=[vst_b])
                    V(lambda e, st=st, tt=tt: e.tensor_copy(out=Vo[:, st * 4 + tt, :], in_=vst[:]), r=[vst_b], w=[Vo_b])
                lvl(0.3)
                for h in range(4):
                    ps, pb = fm_proj(Wk, Wk_b, 8, h * 128, lambda k: hT[:, k, :], [hT_b])
                    A(lambda e, ps=ps, h=h, st=st: e.copy(out=KTo[:, h, st * 512:(st + 1) * 512], in_=ps[:, :]), r=[pb], w=[KTo_b])
                lvl(0.4)
            lvl(0.5)
            kxb = Buf("kx"); vxb = Buf("vx"); gkb = Buf("gk"); gvb = Buf("gv")
            S.dma("sp", lambda e: e.dma_start(out=kx_d[l], in_=KTo[:].rearrange("p h t -> p (h t)")), xch_sem, r=[KTo_b], w=[kxb])
            S.dma("sp", lambda e: e.dma_start(out=vx_d[l], in_=Vo[:].rearrange("p a c -> p (a c)")), xch_sem, r=[Vo_b], w=[vxb])
            kxb.w = vxb.w = (None, xch_sem[0], xch_sem[1])
            if use_cc:
                S.cc(lambda e: e.collective_compute("AllGather", ALU.bypass, replica_groups=PAIRS, ins=[kx_d[l]], outs=[gk_d[l]]),
                     cc_sem, r=[kxb], w=[gkb])
                S.cc(lambda e: e.collective_compute("AllGather", ALU.bypass, replica_groups=PAIRS, ins=[vx_d[l]], outs=[gv_d[l]]),
                     cc_sem, r=[vxb], w=[gvb])
            else:
                S.dma("sp", lambda e: e.dma_start(out=gk_d[l][0:128, :], in_=kx_d[l]), xch_sem, r=[kxb], w=[gkb])
                S.dma("sp", lambda e: e.dma_start(out=gv_d[l][0:128, :], in_=vx_d[l]), xch_sem, r=[vxb], w=[gvb])
                gkb.w = gvb.w = (None, xch_sem[0], xch_sem[1])
            S.dma("sp", lambda e: e.dma_start(out=KTx[:].rearrange("p h t -> p (h t)"), in_=gk_d[l][0:128, :]), xch_sem, r=[gkb], w=[KTx_b])
            S.dma("sp", lambda e: e.dma_start(out=Vx[:].rearrange("p a c -> p (a c)"), in_=gv_d[l][0:128, :]), xch_sem, r=[gvb], w=[Vx_b])
            KTx_b.w = Vx_b.w = (None, xch_sem[0], xch_sem[1])

        def phase_b(l, xsrc, xsrc_b, xdst, xprev_src, xprev_b):
            blkW = lambda i: w_in_d[l, :, i * 512:(i + 1) * 512]
            S.dma("sp", lambda e: e.dma_start(out=xpv[:], in_=xprev_src), pp_sem, r=[xprev_b], w=[xpv_b])
            xpv_b.w = (None, pp_sem[0], pp_sem[1])
            V(lambda e: e.tensor_tensor(out=xpv[:], in0=xpv[:], in1=modp[0:2, D:2 * D], op=ALU.mult), r=[xpv_b, modp_b], w=[xpv_b])
            V(lambda e: e.tensor_tensor(out=hpb[:], in0=xpv[:], in1=modp[0:2, 0:D], op=ALU.add), r=[xpv_b, modp_b], w=[hpb_b])
            for k in range(8):
                S.op("pe", lambda e, k=k: e.transpose(out=PT[:, k * 2:(k + 1) * 2], in_=hpb[0:2, k * 128:(k + 1) * 128],
                                                      identity=identb[0:2, 0:2]), r=[hpb_b, identb_b], w=[PTB])
            V(lambda e: e.tensor_copy(out=hTp[:], in_=PT[:, 0:16].rearrange("p (k t) -> p k t", t=2)), r=[PTB], w=[hTp_b])
            for st in range(4):
                compute_hT(st, xsrc, xsrc_b)
                Wq, Wq_b = wblock(blkW(0), 8)
                for h in range(4):
                    ps, pb = fm_proj(Wq, Wq_b, 8, h * 128, lambda k: hT[:, k, :], [hT_b])
                    A(lambda e, ps=ps, h=h: e.copy(out=qT[:, h, :], in_=ps[:, :]), r=[pb], w=[qT_b])
                Wg, Wg_b = wblock(blkW(3), 8)
                for h in range(4):
                    ps, pb = fm_proj(Wg, Wg_b, 8, h * 128, lambda k: hT[:, k, :], [hT_b])
                    A(lambda e, ps=ps, h=h: e.activation(out=WK[:, 4 + h, :], in_=ps[:, :], func=AF.Silu), r=[pb], w=[WK_b[4 + h]])
                lvl(2)
                tiles = [("x", kt_, None) for kt_ in range(16)]
                for kt_ in range(st * 4 + 4):
                    tiles.append(("o", kt_, kt_ - st * 4 if kt_ >= st * 4 else None))
                cnt = 0
                for h in range(4):
                    for m in range(2):
                        ops_, opb = PS[6], PSB[6]
                        lps_, lpb = PS[3], PSB[3]
                        S.op("pe", lambda e, h=h: e.matmul(ops_[:, :], lhsT=zerosb[:], rhs=qT[:, h, :], start=True, stop=False),
                             r=[zerosb_b, qT_b], w=[opb])
                        S.op("pe", lambda e, h=h: e.matmul(lps_[:, :], lhsT=zerosb[:], rhs=qT[:, h, :], start=True, stop=False),
                             r=[zerosb_b, qT_b], w=[lpb])
                        for ti, (seg, kt_, dj) in enumerate(tiles):
                            last = (ti == len(tiles) - 1)
                            c0 = 0 if dj is None else dj * 128
                            KT, KT_b, VV, VV_b = (KTx, KTx_b, Vx, Vx_b) if seg == "x" else (KTo, KTo_b, Vo, Vo_b)
                            i = cnt % 2
                            cnt += 1
                            sps, spb = PS[4 + i], PSB[4 + i]
                            S.op("pe", lambda e, KT=KT, m=m, h=h, kt_=kt_, c0=c0, sps=sps: e.matmul(
                                sps[:, c0:512], lhsT=KT[m * 64:(m + 1) * 64, h, kt_ * 128:(kt_ + 1) * 128],
                                rhs=qT[m * 64:(m + 1) * 64, h, c0:512], start=True, stop=True), r=[KT_b, qT_b], w=[spb])
                            if seg == "x":
                                A(lambda e, i=i, sps=sps: e.activation(out=Pt[i][:, :], in_=sps[:, :], func=AF.Exp,
                                                                       scale=ATT_SCALE, bias=mbt[:, 0:1]), r=[spb, mbt_b], w=[Pt_b[i]])
                            else:
                                A(lambda e, i=i, sps=sps, c0=c0: e.activation(out=Pt[i][:, c0:512], in_=sps[:, c0:512], func=AF.Exp,
                                                                              scale=ATT_SCALE), r=[spb], w=[Pt_b[i]])
                            if dj is not None:
                                S.op("pool", lambda e, i=i, c0=c0: e.tensor_tensor(out=Pt[i][:, c0:c0 + 128], in0=Pt[i][:, c0:c0 + 128],
                                                                                    in1=trilb[:], op=ALU.mult),
                                     r=[Pt_b[i], trilb_b], w=[Pt_b[i]])
                            S.op("pe", lambda e, VV=VV, kt_=kt_, h=h, i=i, c0=c0, last=last: e.matmul(
                                ops_[:, c0:512], lhsT=VV[:, kt_, h * 128:(h + 1) * 128], rhs=Pt[i][:, c0:512],
                                start=False, stop=last), r=[VV_b, Pt_b[i]], w=[opb])
                            S.op("pe", lambda e, i=i, c0=c0, last=last: e.matmul(
                                lps_[:, c0:512], lhsT=onesb[:], rhs=Pt[i][:, c0:512], start=False, stop=last),
                                r=[onesb_b, Pt_b[i]], w=[lpb])
                        V(lambda e: e.reciprocal(out=osb[2][:], in_=lps_[:, :]), r=[lpb], w=[osb_b[2]])
                        V(lambda e, m=m: e.tensor_tensor(out=osb[m][:], in0=ops_[:, :], in1=osb[2][:], op=ALU.mult),
                          r=[opb, osb_b[2]], w=[osb_b[m]])
                    V(lambda e: e.scalar_tensor_tensor(out=osb[0][:], in0=osb[1][:], scalar=lam2[:, 3:4], in1=osb[0][:],
                                                       op0=ALU.mult, op1=ALU.add), r=[osb_b[0], osb_b[1], lam2_b], w=[osb_b[0]])
                    A(lambda e: e.activation(out=sqb[:], in_=osb[0][:], func=AF.Square), r=[osb_b[0]], w=[sqb_b])
                    ps, pb = next3()
                    S.op("pe", lambda e, ps=ps: e.matmul(ps[:, :], lhsT=onesb[:], rhs=sqb[:], start=True, stop=True),
                         r=[onesb_b, sqb_b], w=[pb])
                    V(lambda e, ps=ps: e.tensor_scalar(out=osb[2][:], in0=ps[:, :], scalar1=1.0 / 128, scalar2=LN_EPS,
                                                       op0=ALU.mult, op1=ALU.add), r=[pb], w=[osb_b[2]])
                    A(lambda e: e.activation(out=osb[2][:], in_=osb[2][:], func=AF.Sqrt), r=[osb_b[2]], w=[osb_b[2]])
                    V(lambda e: e.reciprocal(out=osb[2][:], in_=osb[2][:]), r=[osb_b[2]], w=[osb_b[2]])
                    V(lambda e: e.tensor_tensor(out=osb[0][:], in0=osb[0][:], in1=osb[2][:], op=ALU.mult),
                      r=[osb_b[0], osb_b[2]], w=[osb_b[0]])
                    V(lambda e, h=h: e.scalar_tensor_tensor(out=yT[:, h, :], in0=osb[0][:], scalar=gsc[:, 0:1], in1=WK[:, 4 + h, :],
                                                            op0=ALU.mult, op1=ALU.mult), r=[osb_b[0], gsc_b, WK_b[4 + h]], w=[yT_b])
                lvl(3)
                Wcc, Wcc_b = wblock(blkW(5), 8)
                Wcx, Wcx_b = wblock(blkW(6), 8)
                for c in range(4):
                    ps1, pb1 = fm_proj(Wcc, Wcc_b, 8, c * 128, lambda k: hT[:, k, :], [hT_b])
                    ps2, pb2 = fm_proj(Wcx, Wcx_b, 8, c * 128, lambda k: hT[:, k, :], [hT_b])
                    A(lambda e, ps2=ps2: e.copy(out=osb[2][:], in_=ps2[:, :]), r=[pb2], w=[osb_b[2]])
                    V(lambda e, ps1=ps1, c=c: e.tensor_tensor(out=WK[:, c, :], in0=ps1[:, :], in1=osb[2][:], op=ALU.mult),
                      r=[pb1, osb_b[2]], w=[WK_b[c]])
                    if st == 0:
                        ph1, phb1 = fm_proj(Wcc, Wcc_b, 8, c * 128, lambda k: hTp[:, k, :], [hTp_b], ncols=2)
                        ph2, phb2 = fm_proj(Wcx, Wcx_b, 8, c * 128, lambda k: hTp[:, k, :], [hTp_b], ncols=2)
                        A(lambda e, ph2=ph2: e.copy(out=tmp2[:], in_=ph2[:, 0:2]), r=[phb2], w=[tmp2_b])
                        V(lambda e, ph1=ph1, c=c: e.tensor_tensor(out=ucar[:, c, :], in0=ph1[:, 0:2], in1=tmp2[:], op=ALU.mult),
                          r=[phb1, tmp2_b], w=[ucar_b])
                        V(lambda e, c=c: e.tensor_scalar(out=ucar[:, c, :], in0=ucar[:, c, :], scalar1=hft[:, 0:1], scalar2=None,
                                                         op0=ALU.mult), r=[ucar_b, hft_b], w=[ucar_b])
                for c in range(4):
                    u_ = WK[:, c, :]
                    o_ = WK[:, 4 + c, :]
                    ub, ob = WK_b[c], WK_b[4 + c]
                    V(lambda e, u_=u_, o_=o_, c=c: e.tensor_scalar(out=o_, in0=u_, scalar1=cwc[:, c, 2:3], scalar2=None, op0=ALU.mult),
                      r=[ub, cwc_b], w=[ob])
                    V(lambda e, u_=u_, o_=o_, c=c: e.scalar_tensor_tensor(out=o_[:, 1:512], in0=u_[:, 0:511], scalar=cwc[:, c, 1:2],
                                                                          in1=o_[:, 1:512], op0=ALU.mult, op1=ALU.add),
                      r=[ub, ob, cwc_b], w=[ob])
                    V(lambda e, u_=u_, o_=o_, c=c: e.scalar_tensor_tensor(out=o_[:, 2:512], in0=u_[:, 0:510], scalar=cwc[:, c, 0:1],
                                                                          in1=o_[:, 2:512], op0=ALU.mult, op1=ALU.add),
                      r=[ub, ob, cwc_b], w=[ob])
                    V(lambda e, o_=o_, c=c: e.scalar_tensor_tensor(out=o_[:, 0:1], in0=ucar[:, c, 1:2], scalar=cwc[:, c, 1:2],
                                                                   in1=o_[:, 0:1], op0=ALU.mult, op1=ALU.add),
                      r=[ucar_b, ob, cwc_b], w=[ob])
                    V(lambda e, o_=o_, c=c: e.scalar_tensor_tensor(out=o_[:, 0:2], in0=ucar[:, c, 0:2], scalar=cwc[:, c, 0:1],
                                                                   in1=o_[:, 0:2], op0=ALU.mult, op1=ALU.add),
                      r=[ucar_b, ob, cwc_b], w=[ob])
                    V(lambda e, u_=u_, c=c: e.tensor_copy(out=ucar[:, c, :], in_=u_[:, 510:512]), r=[ub], w=[ucar_b])
                if st == 3:
                    for j in range(2):
                        S.dma("sp", lambda e, j=j: e.dma_start(out=convp_o[l, j, :].rearrange("(c p) -> p c", p=128), in_=ucar[:, :, j]),
                              st_sem, r=[ucar_b])
                Wcb, Wcb_b = wblock(blkW(4), 8)
                for c in range(4):
                    ps, pb = fm_proj(Wcb, Wcb_b, 8, c * 128, lambda k: hT[:, k, :], [hT_b])
                    V(lambda e, ps=ps, c=c: e.tensor_tensor(out=WK[:, 4 + c, :], in0=WK[:, 4 + c, :], in1=ps[:, :], op=ALU.mult),
                      r=[pb, WK_b[4 + c]], w=[WK_b[4 + c]])
                Wgc, Wgc_b = wblock(blkW(7), 8)
                for c in range(4):
                    ps, pb = fm_proj(Wgc, Wgc_b, 8, c * 128, lambda k: hT[:, k, :], [hT_b])
                    A(lambda e, ps=ps: e.activation(out=osb[2][:], in_=ps[:, :], func=AF.Silu), r=[pb], w=[osb_b[2]])
                    V(lambda e, c=c: e.tensor_tensor(out=yT[:, 4 + c, :], in0=WK[:, 4 + c, :], in1=osb[2][:], op=ALU.mult),
                      r=[WK_b[4 + c], osb_b[2]], w=[yT_b])
                lvl(4)
                Wcv, Wcv_b = wblock(blkW(9), 8)
                for tt in range(4):
                    ps, pb = next3()
                    for k in range(8):
                        S.op("pe", lambda e, k=k, ps=ps, tt=tt, Wcv=Wcv: e.matmul(ps[:, :], lhsT=hT[:, k, tt * 128:(tt + 1) * 128], rhs=Wcv[:, k, :],
                                                                                start=(k == 0), stop=(k == 7)), r=[hT_b, Wcv_b], w=[pb])
                    A(lambda e, ps=ps: e.copy(out=cvt[:], in_=ps[:, :]), r=[pb], w=[cvt_b])
                    layer_norm_rows(cvt, cvt_b, 128, 512, clng, clng_b, clnb, clnb_b, vnf, vnf_b, scr_t=ht, scr_b=ht_b)
                    V(lambda e, tt=tt: e.tensor_copy(out=vn[:, tt, :], in_=vnf[:]), r=[vnf_b], w=[vn_b])
                for g in range(4):
                    ps, pb = next3()
                    for tt in range(4):
                        S.op("pe", lambda e, ps=ps, tt=tt, g=g: e.matmul(ps[:, tt * 128:(tt + 1) * 128], lhsT=vn[:, tt, g * 128:(g + 1) * 128],
                                                                       rhs=wsT[:, g, :], start=True, stop=True), r=[vn_b, wsT_b], w=[pb])
                    V(lambda e, ps=ps, g=g: e.tensor_tensor(out=WK[:, g, :].rearrange("p (a t) -> p a t", t=128),
                                                            in0=ps[:, :].rearrange("p (a t) -> p a t", t=128),
                                                            in1=bsb[:, g, :].unsqueeze(1).to_broadcast([128, 4, 128]), op=ALU.add),
                      r=[pb, bsb_b], w=[WK_b[g]])
                if debug and l == 0 and st in (0, 1):
                    S.dma("sp", lambda e, st=st: e.dma_start(out=dbgv_o[st], in_=vn[:].rearrange("p a t -> p (a t)")), st_sem, r=[vn_b])
                    S.dma("sp", lambda e, st=st: e.dma_start(out=dbgx_o[st], in_=WK[:, 0:4, :].rearrange("p a t -> p (a t)")), st_sem, r=WK_b[0:4])
                Wcu, Wcu_b = wblock(blkW(8), 8)
                for g in range(4):
                    ps, pb = fm_proj(Wcu, Wcu_b, 8, g * 128, lambda k: hT[:, k, :], [hT_b])
                    V(lambda e, ps=ps, g=g: e.tensor_tensor(out=WK[:, g, :], in0=WK[:, g, :], in1=ps[:, :], op=ALU.mult),
                      r=[pb, WK_b[g]], w=[WK_b[g]])
                Wgk, Wgk_b = wblock(blkW(10), 8)
                for g in range(4):
                    ps, pb = fm_proj(Wgk, Wgk_b, 8, g * 128, lambda k: hT[:, k, :], [hT_b])
                    A(lambda e, ps=ps: e.activation(out=osb[2][:], in_=ps[:, :], func=AF.Silu), r=[pb], w=[osb_b[2]])
                    V(lambda e, g=g: e.tensor_tensor(out=yT[:, 8 + g, :], in0=WK[:, g, :], in1=osb[2][:], op=ALU.mult),
                      r=[WK_b[g], osb_b[2]], w=[yT_b])
                lvl(5)
                for n in range(3):
                    for cbk in range(2):
                        Wb, Wb_b = wblock(wbr_d[l, n, :, cbk * 512:(cbk + 1) * 512], 4)
                        c0w = 5632 + n * D + cbk * 512
                        Wm, Wm_b = wblock(w_in_d[l, :, c0w:c0w + 512], 8)
                        for j in range(4):
                            f = cbk * 4 + j
                            psg, pbg = fm_proj(Wm, Wm_b, 8, j * 128, lambda k: hT[:, k, :], [hT_b])
                            A(lambda e, psg=psg: e.activation(out=osb[2][:], in_=psg[:, :], func=AF.Sigmoid), r=[pbg], w=[osb_b[2]])
                            psp, pbp = fm_proj(Wb, Wb_b, 4, j * 128, lambda k, n=n: yT[:, n * 4 + k, :], [yT_b])
                            if n == 0:
                                V(lambda e, psp=psp, f=f: e.tensor_tensor(out=WK[:, f, :], in0=psp[:, :], in1=osb[2][:], op=ALU.mult),
                                  r=[pbp, osb_b[2]], w=[WK_b[f]])
                            else:
                                V(lambda e, psp=psp: e.tensor_tensor(out=osb[1][:], in0=psp[:, :], in1=osb[2][:], op=ALU.mult),
                                  r=[pbp, osb_b[2]], w=[osb_b[1]])
                                V(lambda e, f=f: e.tensor_tensor(out=WK[:, f, :], in0=WK[:, f, :], in1=osb[1][:], op=ALU.add),
                                  r=[WK_b[f], osb_b[1]], w=[WK_b[f]])
                if debug and l == 0 and st == 0:
                    S.dma("sp", lambda e: e.dma_start(out=dbgy_o, in_=yT[:].rearrange("p a t -> p (a t)")), st_sem, r=[yT_b])
                    S.dma("sp", lambda e: e.dma_start(out=dbgm_o, in_=WK[:].rearrange("p a t -> p (a t)")), st_sem, r=WK_b)
                for f in range(8):
                    A(lambda e, f=f: e.copy(out=mrgT[:, f, :], in_=WK[:, f, :]), r=[WK_b[f]], w=[mrgT_b])
                lvl(6)
                Wo0, Wo0_b = wblock(wout_d[l, :, 0:512], 8)
                Wo1, Wo1_b = wblock(wout_d[l, :, 512:1024], 8)
                for tt in range(4):
                    tok0 = st * 512 + tt * 128
                    S.dma("sp", lambda e, tok0=tok0: e.dma_start(out=xt[:], in_=xsrc[tok0:tok0 + 128, :]), xt_sem,
                          r=[xsrc_b], w=[xt_b])
                    for cb, (Wo, Wo_b) in enumerate(((Wo0, Wo0_b), (Wo1, Wo1_b))):
                        ps, pb = next3()
                        for k in range(8):
                            S.op("pe", lambda e, k=k, ps=ps, tt=tt, Wo=Wo: e.matmul(ps[:, :], lhsT=mrgT[:, k, tt * 128:(tt + 1) * 128],
                                                                                  rhs=Wo[:, k, :], start=(k == 0), stop=(k == 7)),
                                 r=[mrgT_b, Wo_b], w=[pb])
                        V(lambda e, ps=ps, cb=cb: e.tensor_tensor(out=rr[:, cb * 512:(cb + 1) * 512], in0=ps[:, :],
                                                                  in1=modp[:, 2 * D + cb * 512:2 * D + (cb + 1) * 512], op=ALU.mult),
                          r=[pb, modp_b], w=[rr_b])
                    V(lambda e: e.scalar_tensor_tensor(out=rr[:], in0=xt[:], scalar=ALPHA, in1=rr[:], op0=ALU.mult, op1=ALU.add),
                      r=[xt_b, rr_b], w=[rr_b])
                    layer_norm_rows(rr, rr_b, 128, D, lng, lng_b, lnb, lnb_b, yo, yo_b, scr_t=rr, scr_b=rr_b)
                    if l == 0:
                        S.dma("sp", lambda e, tok0=tok0: e.dma_start(out=xdst[tok0:tok0 + 128, :], in_=yo[:]), x1_sem, r=[yo_b])
                        yo_b.rd[id(x1_sem[0])] = (None, x1_sem[0], x1_sem[1])
                        if st == 3 and tt == 3:
                            S.dma("sp", lambda e: e.dma_start(out=xtl_d.rearrange("(r a) b -> r (a b)", r=2), in_=yo[126:128, :]),
                                  x1_sem, r=[yo_b])
                    else:
                        S.dma("sp", lambda e, tok0=tok0: e.dma_start(out=xdst[tok0:tok0 + 128, :], in_=yo[:]), yo_sem, r=[yo_b])

        def prompt_rest():
            x1_b.w = (None, x1_sem[0], x1_sem[1])
            xtl_b = Buf("xtl"); xtl_b.w = (None, x1_sem[0], x1_sem[1]); gxt_b = Buf("gxt")
            if use_cc:
                S.cc(lambda e: e.collective_compute("AllGather", ALU.bypass, replica_groups=PAIRS, ins=[xtl_d], outs=[gxt_d]),
                     cc_sem, r=[xtl_b], w=[gxt_b])
            else:
                S.dma("sp", lambda e: e.dma_start(out=gxt_d[0:32, :], in_=xtl_d), x1_sem, r=[xtl_b], w=[gxt_b])
            prompt_params(1)
            phase_a(1, x1_d, x1_b)
            phase_b(1, x1_d, x1_b, yp_o, gxt_d[0:32, :].rearrange("(r a) b -> r (a b)", r=2), gxt_b)

        x0_b = Buf("xp_in", const=True)
        xprev0_b = Buf("xprev_in", const=True)
        try:
            lvl(-0.7)
            prompt_params(0)
            lvl(0)
            phase_a(0, xp_d, x0_b)
            lvl(1)
            phase_b(0, xp_d, x0_b, x1_d, xprev_d, xprev0_b)
            lvl(8)
            prompt_rest()
        except _Stop:
            pass

    S.final_wait("sp", S.dsems)
    with nc.allow_non_contiguous_dma(reason="small strided parameter loads"), nc.Block() as block:
        S.emit(block)
    es.close()
    return nc


def make_consts():
    identf = np.eye(128, dtype=np.float32)
    identb = identf.astype(ml_dtypes.bfloat16)
    trilf = np.triu(np.ones((128, 128), np.float32))
    sel = np.zeros((NS, 16, 128), np.float32)
    for pr in range(16):
        sel[2 * pr, pr, 0:64] = 1.0
        sel[2 * pr + 1, pr, 64:128] = 1.0
    return dict(identf=identf, identb=identb, trilf=trilf, sel=sel)


def make_in_maps(inp, n_cores=8):
    cst = make_consts()
    f = lambda a: np.ascontiguousarray(np.asarray(a))
    ck = f(inp["cache_k"]); cv = f(inp["cache_v"])
    n_pool = ck.shape[1]
    ckr = ck.reshape(DEPTH, n_pool, 128, 512)
    cvr = cv.reshape(DEPTH, n_pool, 128, 512)
    shared = dict(
        xs=f(inp["x_sample"]).reshape(NS, D), cs=f(inp["c_sample"]), pt=f(inp["page_table"]).astype(np.int32),
        stc=f(inp["state_conv"]), w_ada=f(inp["w_ada"]), b_ada=f(inp["b_ada"]), w_in=f(inp["w_in"]),
        dlam=f(inp["diff_lambda"]).reshape(DEPTH, 256), subg=f(inp["subln_g"]), convw=f(inp["conv_w"]),
        clng=f(inp["chunk_ln_g"]), clnb=f(inp["chunk_ln_b"]), cws=f(inp["chunk_w_s"]), cbs=f(inp["chunk_b_s"]),
        wbr=f(inp["w_branch"]), wout=f(inp["w_out"]), lng=f(inp["ln_g"]), lnb=f(inp["ln_b"]), **cst)
    xp = f(inp["x_prompt"]); cp = f(inp["c_prompt"])
    maps = []
    for c in range(n_cores):
        b, half = c // 2, c % 2
        m = dict(shared)
        for l in range(DEPTH):
            for u in range(NU):
                r0 = ROWS * c + u * UR
                m[f"ck{l}_{u}"] = np.ascontiguousarray(ckr[l, :, r0:r0 + UR, :]).reshape(n_pool, UR * 512)
                m[f"cv{l}_{u}"] = np.ascontiguousarray(cvr[l, :, r0:r0 + UR, :]).reshape(n_pool, UR * 512)
        m["xp"] = np.ascontiguousarray(xp[b, half * NT:(half + 1) * NT])
        m["cp"] = np.ascontiguousarray(cp[b:b + 1])
        m["mb"] = np.full((128, 1), 0.0 if half == 1 else -30000.0, np.float32)
        m["hf"] = np.full((128, 1), 1.0 if half == 1 else 0.0, np.float32)
        m["xprev"] = np.ascontiguousarray(xp[b, NT - 2:NT]) if half == 1 else np.zeros((2, D), np.float32)
        maps.append(m)
    return maps, n_pool


def assemble(results):
    B, SEQ = 4, 4096
    yp = np.zeros((B, SEQ, D), np.float32)
    kp = np.zeros((DEPTH, B, SEQ, 512), np.float32)
    vp = np.zeros((DEPTH, B, SEQ, 512), np.float32)
    convp = np.zeros((DEPTH, B, 2, 512), np.float32)
    for c in range(8):
        b, half = c // 2, c % 2
        r = results[c]
        yp[b, half * NT:(half + 1) * NT] = r["yp"]
        kp[:, b, half * NT:(half + 1) * NT] = r["kp"]
        vp[:, b, half * NT:(half + 1) * NT] = r["vp"]
        if half == 1:
            convp[:, b] = r["convp"]
    r0 = results[0]
    return (yp, r0["ys"].reshape(NS, 1, D).copy(),
            kp.reshape(DEPTH, B, SEQ, 4, 2, 64), vp.reshape(DEPTH, B, SEQ, 4, 128), convp,
            r0["ks"].reshape(DEPTH, NS, 1, 4, 2, 64).copy(), r0["vs"].reshape(DEPTH, NS, 1, 4, 128).copy(),
            r0["convs"].copy(), r0["cvs"].reshape(DEPTH, NS, 1, 512).copy())


PROMPT_ENABLED = True


def kernel(**inputs):
    maps, n_pool = make_in_maps(inputs)
    nc = build_nc(n_pool=n_pool, do_prompt=PROMPT_ENABLED)
    res = run_bass_kernel_spmd(nc, maps, core_ids=list(range(8)))
    return assemble(res.results)
```

```python
import math
from contextlib import ExitStack

import numpy as np
import ml_dtypes
import concourse.bass as bass
import concourse.mybir as mybir
from concourse.bass_utils import run_bass_kernel_spmd

F32 = mybir.dt.float32
BF16 = mybir.dt.bfloat16
I32 = mybir.dt.int32
AF = mybir.ActivationFunctionType
ALU = mybir.AluOpType
AX = mybir.AxisListType

D = 1024
DEPTH = 2
NS = 32
NPAGES = 64
ROWS = 16
UR = 2
NU = ROWS // UR
NT = 2048
D_IN = 8704
ALPHA = (2.0 * DEPTH) ** 0.25
LN_EPS = 1e-5
ATT_SCALE = 64 ** -0.5


def lambda_init(l):
    return 0.8 - 0.6 * math.exp(-0.3 * l)


class Buf:
    __slots__ = ("name", "w", "rd", "const", "strict")

    def __init__(self, name, const=False, strict=False):
        self.name = name
        self.w = None
        self.rd = {}
        self.const = const
        self.strict = strict


class Sched:
    ENGS = ("pe", "act", "dve", "pool", "sp")

    def __init__(self, nc, es):
        self.nc = nc
        self.es = es
        self.ops = {k: [] for k in self.ENGS}
        self.esem = {k: es.enter_context(nc.semaphore("s_" + k)) for k in ("pe", "act", "dve", "pool")}
        self.cnt = {k: 0 for k in self.esem}
        self.waited = {k: {} for k in self.ENGS}
        self.nsem = 0
        self.dsems = []

    def newsem(self, name):
        self.nsem += 1
        ds = [self.es.enter_context(self.nc.semaphore(name)), 0]
        self.dsems.append(ds)
        return ds

    def barrier(self):
        for eng in self.ENGS:
            waits = []
            for k in ("pe", "act", "dve", "pool"):
                if k != eng and self.cnt[k] > 0:
                    waits.append((self.esem[k], self.cnt[k]))
            for ds in self.dsems:
                if ds[1] > 0:
                    waits.append((ds[0], ds[1]))
            for sem, val in waits:
                self.waited[eng][id(sem)] = max(self.waited[eng].get(id(sem), 0), val)
            self.ops[eng].append((waits, None, None, 0))

    def _waits(self, eng, reads, writes):
        toks = []
        for b in reads:
            if b.w is not None:
                toks.append(b.w + (b.strict,))
        for b in writes:
            if b.w is not None:
                toks.append(b.w + (b.strict,))
            toks.extend(t + (b.strict,) for t in b.rd.values())
        waits = []
        wd = self.waited[eng]
        for (teng, sem, val, strict) in toks:
            if teng == eng and not (strict and eng != "pe"):
                continue
            k = id(sem)
            if wd.get(k, 0) >= val:
                continue
            wd[k] = val
            waits.append((sem, val))
        return waits

    def _mark(self, tok, key, reads, writes):
        for b in reads:
            if not b.const:
                b.rd[key] = tok
        for b in writes:
            b.w = tok
            b.rd = {}

    def op(self, eng, fn, r=(), w=()):
        waits = self._waits(eng, r, w)
        self.cnt[eng] += 1
        sem = self.esem[eng]
        tok = (eng, sem, self.cnt[eng])
        self.ops[eng].append((waits, fn, sem, 1))
        self._mark(tok, eng, r, w)
        return tok

    def dma(self, q, fn, dsem, r=(), w=()):
        waits = self._waits(q, r, w)
        dsem[1] += 16
        tok = (None, dsem[0], dsem[1])
        self.ops[q].append((waits, fn, dsem[0], 16))
        self._mark(tok, id(dsem[0]), r, w)
        return tok

    def cc(self, fn, csem, r=(), w=()):
        waits = self._waits("pool", r, w)
        csem[1] += 1
        tok = (None, csem[0], csem[1])
        self.ops["pool"].append((waits, fn, csem[0], 1))
        self._mark(tok, id(csem[0]), r, w)
        return tok

    def final_wait(self, q, sems):
        for ds in sems:
            if ds[1] > 0:
                self.ops[q].append(([(ds[0], ds[1])], None, None, 0))

    def emit(self, block):
        def mk(key):
            def f(eng):
                for waits, fn, sem, inc in self.ops[key]:
                    for ws, wv in waits:
                        eng.wait_ge(ws, wv)
                    if fn is not None:
                        fn(eng).then_inc(sem, inc)
            return f
        block.tensor(mk("pe"))
        block.scalar(mk("act"))
        block.vector(mk("dve"))
        block.gpsimd(mk("pool"))
        block.sync(mk("sp"))


class Ctx:
    pass


class _Stop(Exception):
    pass


def build_nc(n_pool=2560, do_prompt=True, do_sample=True, use_cc=True, skip_attn=False, debug=False, plevel=99):
    nc = bass.Bass("TRN2", target_bir_lowering=False)
    es = ExitStack()
    S = Sched(nc, es)
    c = Ctx()

    def din(name, shape, dt=F32):
        return nc.dram_tensor(name, list(shape), dt, kind="ExternalInput").ap()

    def dout(name, shape, dt=F32):
        return nc.dram_tensor(name, list(shape), dt, kind="ExternalOutput").ap()

    def dint(name, shape, dt=F32):
        return nc.dram_tensor(name, list(shape), dt, kind="Internal").ap()

    xs_d = din("xs", [NS, D]); cs_d = din("cs", [NS, D])
    pt_d = din("pt", [NS, NPAGES], I32)
    ck_d = [[din(f"ck{l}_{u}", [n_pool, UR * 512]) for u in range(NU)] for l in range(DEPTH)]
    cv_d = [[din(f"cv{l}_{u}", [n_pool, UR * 512]) for u in range(NU)] for l in range(DEPTH)]
    stc_d = din("stc", [DEPTH, NS, 2, 512])
    w_ada_d = din("w_ada", [DEPTH, D, 3 * D]); b_ada_d = din("b_ada", [DEPTH, 3 * D])
    w_in_d = din("w_in", [DEPTH, D, D_IN])
    dlam_d = din("dlam", [DEPTH, 256]); subg_d = din("subg", [DEPTH, 128])
    convw_d = din("convw", [DEPTH, 3, 512])
    clng_d = din("clng", [DEPTH, 512]); clnb_d = din("clnb", [DEPTH, 512])
    cws_d = din("cws", [DEPTH, 4, 128, 128]); cbs_d = din("cbs", [DEPTH, 4, 128])
    wbr_d = din("wbr", [DEPTH, 3, 512, D]); wout_d = din("wout", [DEPTH, D, D])
    lng_d = din("lng", [DEPTH, D]); lnb_d = din("lnb", [DEPTH, D])
    xp_d = din("xp", [NT, D]); cp_d = din("cp", [1, D]); mb_d = din("mb", [128, 1])
    hf_d = din("hf", [128, 1]); xprev_d = din("xprev", [2, D])
    identb_d = din("identb", [128, 128], BF16); identf_d = din("identf", [128, 128])
    trilf_d = din("trilf", [128, 128]); sel_d = din("sel", [NS, 16, 128])

    ys_o = dout("ys", [NS, D]); ks_o = dout("ks", [DEPTH, NS, 512]); vs_o = dout("vs", [DEPTH, NS, 512])
    convs_o = dout("convs", [DEPTH, NS, 2, 512]); cvs_o = dout("cvs", [DEPTH, NS, 512])
    yp_o = dout("yp", [NT, D]); kp_o = dout("kp", [DEPTH, NT, 512]); vp_o = dout("vp", [DEPTH, NT, 512])
    convp_o = dout("convp", [DEPTH, 2, 512])

    part_d = [dint(f"part{l}", [128, 512]) for l in range(DEPTH)]
    part_v = [p.rearrange("(a b) c -> a (b c)", b=2) for p in part_d]
    g4_d = [dint(f"g4_{l}", [4 * 128, 512]) for l in range(DEPTH)]
    g4b_d = [dint(f"g4b_{l}", [4 * 128, 512]) for l in range(DEPTH)]
    g8_d = [dint(f"g8_{l}", [8 * 128, 512]) for l in range(DEPTH)]

    es_s = ExitStack()
    cur = [es]

    def sb(name, shape, dt=F32):
        t = cur[0].enter_context(nc.sbuf_tensor("sb_" + name, list(shape), dt))
        return t

    def psb(name, shape, dt=F32):
        return es.enter_context(nc.psum_tensor(name, list(shape), dt))

    st_sem = S.newsem("st")
    dbg_o = dout("dbg", [12, 128, 1024]) if debug else None
    dbgy_o = dout("dbgy", [128, 12 * 512], BF16) if debug else None
    dbgm_o = dout("dbgm", [128, 8 * 512]) if debug else None
    dbgv_o = dout("dbgv", [2, 128, 4 * 512], BF16) if debug else None
    dbgx_o = dout("dbgx", [2, 128, 4 * 512]) if debug else None

    def dump(slot, ap, buf, rows, cols):
        if debug:
            S.dma("sp", lambda e: e.dma_start(out=dbg_o[slot, 0:rows, 0:cols], in_=ap), st_sem, r=[buf])

    cst_sem = S.newsem("cst")
    cc_sem = S.newsem("ccs")

    PS = [psb(f"ps{i}", [128, 512], F32) for i in range(7)]
    PSB = [Buf(f"ps{i}") for i in range(7)]
    PT = psb("pst", [128, 512], BF16)
    PTB = Buf("pst")

    consts = []

    def cload(name, shape, src, dt=F32, q="sp", slow=False):
        t = sb(name, shape, dt)
        b = Buf(name, const=True)
        S.dma(q, lambda e, t=t, src=src: e.dma_start(out=t[:], in_=src, allow_slow_non_contiguous=slow), cst_sem, w=[b])
        consts.append(b)
        return t, b

    identb, identb_b = cload("identb", [128, 128], identb_d, BF16)
    identf, identf_b = cload("identf", [128, 128], identf_d)
    trilf, trilf_b = cload("trilf", [128, 128], trilf_d)
    ones_f = sb("ones_f", [128, 128]); ones_b = Buf("ones_f", const=True)
    S.op("pool", lambda e: e.memset(ones_f[:], 1.0), w=[ones_b])
    zeros_f = sb("zeros_f", [128, 16]); zeros_b = Buf("zeros_f", const=True)
    S.op("pool", lambda e: e.memset(zeros_f[:], 0.0), w=[zeros_b])

    NSTG, NWB = 2, 2
    stg = [sb(f"wstg{i}", [128, 4, 512]) for i in range(NSTG)]
    stg_b = [Buf(f"wstg{i}") for i in range(NSTG)]
    stg_sem = [S.newsem(f"wstg_s{i}") for i in range(NSTG)]
    wbf = [sb(f"wbf{i}", [128, 8, 512], BF16) for i in range(NWB)]
    wbf_b = [Buf(f"wbf{i}") for i in range(NWB)]
    wctr = [0, 0]

    def wblock(src2d, kc):
        j = wctr[1] % NWB; wctr[1] += 1
        srcv = src2d.rearrange("(kc p) n -> p kc n", p=128)
        for k0 in range(0, kc, 4):
            i = wctr[0] % NSTG; wctr[0] += 1
            S.dma("sp", lambda e, i=i, srcv=srcv, k0=k0: e.dma_start(out=stg[i][:, :, :], in_=srcv[:, k0:k0 + 4, :]),
                  stg_sem[i], w=[stg_b[i]])
            S.op("pool", lambda e, i=i, j=j, k0=k0: e.tensor_copy(out=wbf[j][:, k0:k0 + 4, :], in_=stg[i][:, :, :]),
                 r=[stg_b[i]], w=[wbf_b[j]])
        return wbf[j], wbf_b[j]

    psrr = [0]

    def next_ps():
        i = psrr[0] % 4
        psrr[0] += 1
        return PS[i], PSB[i]

    lamt = sb("lamt", [128, 256]); lamt_b = Buf("lamt")
    lam2 = sb("lam2", [128, 4]); lam2_b = Buf("lam2", strict=True)
    subg = sb("subg", [128, 128]); subg_b = Buf("subg")
    clng = sb("clng", [128, 512]); clng_b = Buf("clng")
    clnb = sb("clnb", [128, 512]); clnb_b = Buf("clnb")
    ws00 = sb("ws00", [128, 4]); ws00_b = Buf("ws00")
    bs0 = sb("bs0", [128, 4]); bs0_b = Buf("bs0")
    lng = sb("lng", [128, D]); lng_b = Buf("lng")
    lnb = sb("lnb", [128, D]); lnb_b = Buf("lnb")

    mv = sb("mv", [128, 2]); mv_b = Buf("mv", strict=True)
    small_sem = S.newsem("small")
    trilb = sb("trilb", [128, 128], BF16); trilb_b = Buf("trilb", const=True)
    onesb = sb("onesb", [128, 128], BF16); onesb_b = Buf("onesb", const=True)
    zerosb = sb("zerosb", [128, 128], BF16); zerosb_b = Buf("zerosb", const=True)
    S.op("pool", lambda e: e.memset(onesb[:], 1.0), w=[onesb_b])
    S.op("pool", lambda e: e.memset(zerosb[:], 0.0), w=[zerosb_b])

    cur[0] = es_s
    sel, sel_b = cload("sel", [NS, 16, 128], sel_d)
    pidx, pidx_b = cload("pidx", [128, 16], pt_d.rearrange("(pr two) j -> (two j) pr", two=2), I32, slow=True)
    convw = sb("convw", [128, 3, 512]); convw_b = Buf("convw")
    xs_t = sb("xs_t", [NS, D]); xs_b = Buf("xs_t")
    S.dma("sp", lambda e: e.dma_start(out=xs_t[:], in_=xs_d), cst_sem, w=[xs_b])
    cs_t = sb("cs_t", [NS, D]); cs_b = Buf("cs_t")
    S.dma("sp", lambda e: e.dma_start(out=cs_t[:], in_=cs_d), cst_sem, w=[cs_b])
    tot = cst_sem[1]
    for b in consts + [xs_b, cs_b]:
        b.w = (None, cst_sem[0], tot)

    scb = sb("scb", [NS, D], BF16); scb_b = Buf("scb")
    S.op("act", lambda e: e.activation(out=scb[:], in_=cs_t[:], func=AF.Silu), r=[cs_b], w=[scb_b])
    scT = sb("scT", [128, 8, NS], BF16); scT_b = Buf("scT")

    def transpose_rows(src_t, src_b, dst_t, dst_b, nchunks, rows):
        for k in range(nchunks):
            S.op("pe", lambda e, k=k: e.transpose(out=PT[:, k * rows:(k + 1) * rows],
                                                  in_=src_t[0:rows, k * 128:(k + 1) * 128],
                                                  identity=identb[0:rows, 0:rows]),
                 r=[src_b, identb_b], w=[PTB])
        S.op("dve", lambda e: e.tensor_copy(out=dst_t[:, 0:nchunks, :],
                                            in_=PT[:, 0:nchunks * rows].rearrange("p (k r) -> p k r", r=rows)),
             r=[PTB], w=[dst_b])

    transpose_rows(scb, scb_b, scT, scT_b, 8, NS)

    mod_s = sb("mod_s", [NS, 3 * D]); mod_b = Buf("mod_s")
    hsb = sb("hsb", [NS, D], BF16); hsb_b = Buf("hsb")
    hsT = sb("hsT", [128, 8, NS], BF16); hsT_b = Buf("hsT")
    zs = sb("zs", [NS, D_IN]); zs_b = Buf("zs")
    sg = sb("sg", [NS, 3, 512]); sg_b = Buf("sg")

    kt = [sb(f"kt{i}", [128, UR * 512]) for i in range(2)]
    kt_b = [Buf(f"kt{i}") for i in range(2)]
    kt_sem = [S.newsem(f"kts{i}") for i in range(2)]
    vt = [sb(f"vt{i}", [128, UR * 512]) for i in range(2)]
    vt_b = [Buf(f"vt{i}") for i in range(2)]
    vt_sem = [S.newsem(f"vts{i}") for i in range(2)]
    prod = sb("prod", [128, UR * 512]); prod_b = Buf("prod")
    qbc = sb("qbc", [128, 512]); qbc_b = Buf("qbc")
    sc_t = sb("sc_t", [128, UR * 8]); sc_b = Buf("sc_t")
    pbd = [sb(f"pbd{i}", [128, UR, 16]) for i in range(2)]
    pbd_b = [Buf(f"pbd{i}") for i in range(2)]
    for i in range(2):
        S.op("pool", lambda e, i=i: e.memset(pbd[i][:], 0.0), w=[pbd_b[i]])
    partt = [sb(f"partt{i}", [4, 1024]) for i in range(1)]
    partt_b = [Buf(f"partt{i}") for i in range(1)]
    part_sem = S.newsem("part")
    G = sb("G", [NS, 2, 2, 520]); G_b = Buf("G")
    g_sem = S.newsem("gsem")
    acc = sb("acc", [NS, 2, 512]); acc_b = Buf("acc")
    lsum = sb("lsum", [NS, 2, 4]); lsum_b = Buf("lsum", strict=True)
    tmpA = sb("tmpA", [NS, 1024]); tmpA_b = Buf("tmpA")
    tmpB = sb("tmpB", [NS, 1024]); tmpB_b = Buf("tmpB")
    sm = sb("sm", [NS, 64]); sm_b = Buf("sm", strict=True)
    att = sb("att", [NS, 512]); att_b = Buf("att")
    Y = sb("Y", [NS, 3, 512]); Y_b = Buf("Y")
    Yb = sb("Yb", [NS, 3 * 512], BF16); Yb_b = Buf("Yb")
    YT = sb("YT", [128, 12, NS], BF16); YT_b = Buf("YT")
    stc_t = sb("stc_t", [NS, 2, 512]); stc_b = Buf("stc_t")
    convo = sb("convo", [NS, 2, 512]); convo_b = Buf("convo")
    vn_s = sb("vn_s", [NS, 512]); vn_sb = Buf("vn_s")
    stats = sb("stats", [128, 4, 6]); stats_b = Buf("stats")
    merged = sb("merged", [NS, D]); merged_b = Buf("merged")
    mergb = sb("mergb", [NS, D], BF16); mergb_b = Buf("mergb")
    mT = sb("mT", [128, 8, NS], BF16); mT_b = Buf("mT")
    res = sb("res", [NS, D]); res_b = Buf("res")

    def V(fn, r=(), w=()):
        return S.op("dve", fn, r=r, w=w)

    def A(fn, r=(), w=()):
        return S.op("act", fn, r=r, w=w)

    def layer_norm_rows(x_t, x_b, rows, width, g_t, g_b, b_t, b_b, out_t, out_b, scr_t=None, scr_b=None):
        if scr_t is None:
            scr_t, scr_b = tmpA, tmpA_b
        V(lambda e: e.tensor_reduce(out=mv[0:rows, 0:1], in_=x_t[0:rows, 0:width], axis=AX.X, op=ALU.add),
          r=[x_b], w=[mv_b])
        V(lambda e: e.tensor_scalar(out=mv[0:rows, 0:1], in0=mv[0:rows, 0:1], scalar1=1.0 / width, scalar2=None,
                                    op0=ALU.mult), r=[mv_b], w=[mv_b])
        V(lambda e: e.tensor_scalar(out=out_t[0:rows, 0:width], in0=x_t[0:rows, 0:width],
                                    scalar1=mv[0:rows, 0:1], scalar2=None, op0=ALU.subtract), r=[x_b, mv_b], w=[out_b])
        V(lambda e: e.tensor_tensor(out=scr_t[0:rows, 0:width], in0=out_t[0:rows, 0:width], in1=out_t[0:rows, 0:width],
                                    op=ALU.mult), r=[out_b], w=[scr_b])
        V(lambda e: e.tensor_reduce(out=mv[0:rows, 1:2], in_=scr_t[0:rows, 0:width], axis=AX.X, op=ALU.add),
          r=[scr_b], w=[mv_b])
        V(lambda e: e.tensor_scalar(out=mv[0:rows, 1:2], in0=mv[0:rows, 1:2], scalar1=1.0 / width, scalar2=LN_EPS,
                                    op0=ALU.mult, op1=ALU.add), r=[mv_b], w=[mv_b])
        A(lambda e: e.activation(out=mv[0:rows, 1:2], in_=mv[0:rows, 1:2], func=AF.Sqrt), r=[mv_b], w=[mv_b])
        V(lambda e: e.reciprocal(out=mv[0:rows, 1:2], in_=mv[0:rows, 1:2]), r=[mv_b], w=[mv_b])
        V(lambda e: e.scalar_tensor_tensor(out=out_t[0:rows, 0:width], in0=out_t[0:rows, 0:width], scalar=mv[0:rows, 1:2],
                                           in1=g_t[0:rows, 0:width], op0=ALU.mult, op1=ALU.mult),
          r=[out_b, mv_b, g_b], w=[out_b])
        V(lambda e: e.tensor_tensor(out=out_t[0:rows, 0:width], in0=out_t[0:rows, 0:width],
                                    in1=b_t[0:rows, 0:width], op=ALU.add), r=[out_b, b_b], w=[out_b])

    def load_layer_params(l, sample=True):
        def ld(t, b, src):
            S.dma("sp", lambda e: e.dma_start(out=t, in_=src), small_sem, w=[b])
        ld(lamt[:], lamt_b, dlam_d[l, :].partition_broadcast(128))
        ld(subg[:], subg_b, subg_d[l, :].partition_broadcast(128))
        if sample:
            ld(convw[:].rearrange("p j c -> p (j c)"), convw_b,
               convw_d[l].rearrange("j c -> (j c)").partition_broadcast(128))
        ld(clng[:], clng_b, clng_d[l, :].partition_broadcast(128))
        ld(clnb[:], clnb_b, clnb_d[l, :].partition_broadcast(128))
        ld(ws00[:], ws00_b, cws_d[l, :, 0, 0].partition_broadcast(128))
        ld(bs0[:], bs0_b, cbs_d[l, :, 0].partition_broadcast(128))
        ld(lng[:], lng_b, lng_d[l, :].partition_broadcast(128))
        ld(lnb[:], lnb_b, lnb_d[l, :].partition_broadcast(128))
        if sample:
            ld(mod_s[:], mod_b, b_ada_d[l, :].partition_broadcast(NS))
            ld(stc_t[:], stc_b, stc_d[l])
        tot = small_sem[1]
        for b in (lamt_b, subg_b, clng_b, clnb_b, ws00_b, bs0_b, lng_b, lnb_b) + ((convw_b, mod_b, stc_b) if sample else ()):
            b.w = (None, small_sem[0], tot)
        V(lambda e: e.tensor_tensor(out=lamt[:, 0:64], in0=lamt[:, 0:64], in1=lamt[:, 64:128], op=ALU.mult),
          r=[lamt_b], w=[lamt_b])
        V(lambda e: e.tensor_tensor(out=lamt[:, 128:192], in0=lamt[:, 128:192], in1=lamt[:, 192:256], op=ALU.mult),
          r=[lamt_b], w=[lamt_b])
        V(lambda e: e.tensor_reduce(out=lam2[:, 0:2], in_=lamt[:].rearrange("p (a b c) -> p a (b c)", a=2, b=2)[:, :, 0:64],
                                    axis=AX.X, op=ALU.add), r=[lamt_b], w=[lam2_b])
        A(lambda e: e.activation(out=lam2[:, 0:2], in_=lam2[:, 0:2], func=AF.Exp), r=[lam2_b], w=[lam2_b])
        V(lambda e: e.tensor_tensor(out=lam2[:, 2:3], in0=lam2[:, 0:1], in1=lam2[:, 1:2], op=ALU.subtract),
          r=[lam2_b], w=[lam2_b])
        V(lambda e: e.tensor_scalar(out=lam2[:, 2:3], in0=lam2[:, 2:3], scalar1=lambda_init(l), scalar2=None,
                                    op0=ALU.add), r=[lam2_b], w=[lam2_b])
        V(lambda e: e.tensor_scalar(out=lam2[:, 3:4], in0=lam2[:, 2:3], scalar1=-1.0, scalar2=None,
                                    op0=ALU.mult), r=[lam2_b], w=[lam2_b])
        V(lambda e: e.tensor_scalar(out=subg[:], in0=subg[:], scalar1=1.0 - lambda_init(l), scalar2=None,
                                    op0=ALU.mult), r=[subg_b], w=[subg_b])

    def sample_layer(l):
        load_layer_params(l)
        for blk in range(6):
            wt, wb = wblock(w_ada_d[l, :, blk * 512:(blk + 1) * 512], 8)
            ps, psb_ = next_ps()
            for k in range(8):
                S.op("pe", lambda e, k=k, wt=wt, ps=ps: e.matmul(ps[0:NS, :], lhsT=scT[:, k, :], rhs=wt[:, k, :],
                                                              start=(k == 0), stop=(k == 7)),
                     r=[scT_b, wb], w=[psb_])
            V(lambda e, ps=ps, blk=blk: e.tensor_tensor(out=mod_s[:, blk * 512:(blk + 1) * 512], in0=ps[0:NS, :],
                                                        in1=mod_s[:, blk * 512:(blk + 1) * 512], op=ALU.add),
              r=[psb_, mod_b], w=[mod_b])
        V(lambda e: e.scalar_tensor_tensor(out=res[:], in0=mod_s[:, D:2 * D], scalar=1.0, in1=xs_t[:],
                                           op0=ALU.add, op1=ALU.mult), r=[mod_b, xs_b], w=[res_b])
        V(lambda e: e.tensor_tensor(out=hsb[:], in0=res[:], in1=mod_s[:, 0:D], op=ALU.add),
          r=[res_b, mod_b], w=[hsb_b])
        transpose_rows(hsb, hsb_b, hsT, hsT_b, 8, NS)
        for blk in range(17):
            wt, wb = wblock(w_in_d[l, :, blk * 512:(blk + 1) * 512], 8)
            ps, psb_ = next_ps()
            for k in range(8):
                S.op("pe", lambda e, k=k, wt=wt, ps=ps: e.matmul(ps[0:NS, :], lhsT=hsT[:, k, :], rhs=wt[:, k, :],
                                                              start=(k == 0), stop=(k == 7)),
                     r=[hsT_b, wb], w=[psb_])
            A(lambda e, ps=ps, blk=blk: e.copy(out=zs[:, blk * 512:(blk + 1) * 512], in_=ps[0:NS, :]),
              r=[psb_], w=[zs_b])
        S.dma("sp", lambda e: e.dma_start(out=ks_o[l], in_=zs[:, 512:1024]), st_sem, r=[zs_b])
        S.dma("sp", lambda e: e.dma_start(out=vs_o[l], in_=zs[:, 1024:1536]), st_sem, r=[zs_b])

        u = 0
        for pr in range(16):
            qps, qpsb = PS[4], PSB[4]
            S.op("pe", lambda e, pr=pr: e.matmul(qps[:, :], lhsT=sel[:, pr, :], rhs=zs[:, 0:512], start=True, stop=True),
                 r=[sel_b, zs_b], w=[qpsb])
            A(lambda e: e.copy(out=qbc[:], in_=qps[:, :]), r=[qpsb], w=[qbc_b])
            aps, apsb = PS[5], PSB[5]
            lps, lpsb = PS[6], PSB[6]
            S.op("pe", lambda e: e.matmul(aps[0:4, :], lhsT=zeros_f[:, 0:4], rhs=qbc[:, :], start=True, stop=False),
                 r=[zeros_b, qbc_b], w=[apsb])
            S.op("pe", lambda e: e.matmul(lps[0:4, 0:8], lhsT=zeros_f[:, 0:4], rhs=ones_f[:, 0:8], start=True, stop=False),
                 r=[zeros_b, ones_b], w=[lpsb])
            for half in range(0 if skip_attn else NU):
                i = u % 2
                u += 1
                c0 = half * UR * 512
                S.dma("pool", lambda e, i=i, pr=pr, half=half: e.indirect_dma_start(
                    out=kt[i][:], out_offset=None, in_=ck_d[l][half],
                    in_offset=bass.IndirectOffsetOnAxis(ap=pidx[:, pr:pr + 1], axis=0)),
                    kt_sem[i], r=[pidx_b], w=[kt_b[i]])
                S.dma("pool", lambda e, i=i, pr=pr, half=half: e.indirect_dma_start(
                    out=vt[i][:], out_offset=None, in_=cv_d[l][half],
                    in_offset=bass.IndirectOffsetOnAxis(ap=pidx[:, pr:pr + 1], axis=0)),
                    vt_sem[i], r=[pidx_b], w=[vt_b[i]])
                V(lambda e, i=i: e.tensor_tensor(out=prod[:].rearrange("p (r c) -> p r c", c=512),
                                                 in0=kt[i][:].rearrange("p (r c) -> p r c", c=512),
                                                 in1=qbc[:].unsqueeze(1).to_broadcast([128, UR, 512]), op=ALU.mult),
                  r=[kt_b[i], qbc_b], w=[prod_b])
                V(lambda e: e.tensor_reduce(out=sc_t[:], in_=prod[:].rearrange("p (x d) -> p x d", d=64),
                                            axis=AX.X, op=ALU.add), r=[prod_b], w=[sc_b])
                for s2 in range(2):
                    A(lambda e, i=i, s2=s2: e.activation(
                        out=pbd[i][s2 * 64:(s2 + 1) * 64].rearrange("p r (h s m) -> p r h s m", h=4, s=2)[:, :, :, s2, :],
                        in_=sc_t[s2 * 64:(s2 + 1) * 64, :].rearrange("p (r h m) -> p r h m", h=4, m=2),
                        func=AF.Exp, scale=ATT_SCALE), r=[sc_b], w=[pbd_b[i]])
                if l == 0 and pr == 0 and half == 0:
                    dump(0, kt[i][:, 0:1024], kt_b[i], 128, 1024)
                    dump(1, qbc[:], qbc_b, 128, 512)
                    dump(2, sc_t[:], sc_b, 128, UR * 8)
                    dump(3, pbd[i][:].rearrange("p r c -> p (r c)"), pbd_b[i], 128, UR * 16)
                    dump(4, vt[i][:, 0:1024], vt_b[i], 128, 1024)
                for r_ in range(UR):
                    first = (half == 0 and r_ == 0)
                    last = (half == NU - 1 and r_ == UR - 1)
                    for h in range(4):
                        S.op("pe", lambda e, i=i, r_=r_, h=h, first=first, last=last: e.matmul(
                            aps[0:4, h * 128:(h + 1) * 128], lhsT=pbd[i][:, r_, h * 4:(h + 1) * 4],
                            rhs=vt[i][:, r_ * 512 + h * 128:r_ * 512 + (h + 1) * 128],
                            start=False, stop=last), r=[pbd_b[i], vt_b[i]], w=[apsb])
                        S.op("pe", lambda e, i=i, r_=r_, h=h, first=first, last=last: e.matmul(
                            lps[0:4, h * 2:(h + 1) * 2], lhsT=pbd[i][:, r_, h * 4:(h + 1) * 4], rhs=ones_f[:, 0:2],
                            start=False, stop=last), r=[pbd_b[i], ones_b], w=[lpsb])
            j = 0
            if skip_attn:
                V(lambda e, j=j: e.memset(partt[j][:, 0:520], 1.0), w=[partt_b[j]])
            else:
                V(lambda e, j=j: e.tensor_copy(out=partt[j][:, 0:512], in_=aps[0:4, :]), r=[apsb], w=[partt_b[j]])
                V(lambda e, j=j: e.tensor_copy(out=partt[j][:, 512:520], in_=lps[0:4, 0:8]), r=[lpsb], w=[partt_b[j]])
            S.dma("sp", lambda e, j=j, pr=pr: e.dma_start(out=part_v[l][pr * 4:(pr + 1) * 4, :], in_=partt[j][:, :]),
                  part_sem, r=[partt_b[j]])
            if l == 0 and pr == 0:
                dump(5, partt[j][:, :], partt_b[j], 4, 1024)
        pd_b = Buf("part_d"); pd_b.w = (None, part_sem[0], part_sem[1])
        g8_b = Buf("g8")
        if not use_cc:
            S.dma("sp", lambda e: e.dma_start(out=g8_d[l][0:128, :], in_=part_d[l]), part_sem, r=[pd_b], w=[g8_b])
        if use_cc:
            g4_b = Buf("g4"); g4b_b = Buf("g4b")
            S.cc(lambda e: e.collective_compute("AllGather", ALU.bypass, replica_groups=[[0, 1, 2, 3], [4, 5, 6, 7]],
                                                ins=[part_d[l]], outs=[g4_d[l]]), cc_sem, r=[pd_b], w=[g4_b])
            S.dma("pool", lambda e: e.dma_start(out=g4b_d[l], in_=g4_d[l]), part_sem, r=[g4_b], w=[g4b_b])
            S.cc(lambda e: e.collective_compute("AllGather", ALU.bypass, replica_groups=[[0, 4], [1, 5], [2, 6], [3, 7]],
                                                ins=[g4b_d[l]], outs=[g8_d[l]]), cc_sem, r=[g4b_b], w=[g8_b])
        g8v = g8_d[l].rearrange("(a b) c -> a (b c)", b=2).rearrange("(k s m) c -> s k m c", k=8, m=2)
        for kh in range(4):
            for k2 in range(2):
                for m in range(2):
                    S.dma("sp", lambda e, kh=kh, k2=k2, m=m: e.dma_start(out=G[:, k2, m, :], in_=g8v[:, kh * 2 + k2, m, 0:520]),
                          g_sem, r=[g8_b], w=[G_b])
            G_b.w = (None, g_sem[0], g_sem[1])
            Gl = G[:, :, :, 512:520].rearrange("s k m (h t) -> s m h t k", t=2)[:, :, :, 0, :]
            if kh == 0:
                V(lambda e: e.tensor_reduce(out=acc[:], in_=G[:, :, :, 0:512].rearrange("s k m c -> s m c k"),
                                            axis=AX.X, op=ALU.add), r=[G_b], w=[acc_b])
                V(lambda e, Gl=Gl: e.tensor_reduce(out=lsum[:], in_=Gl, axis=AX.X, op=ALU.add), r=[G_b], w=[lsum_b])
            else:
                V(lambda e: e.tensor_reduce(out=tmpA[:].rearrange("s (m c) -> s m c", m=2),
                                            in_=G[:, :, :, 0:512].rearrange("s k m c -> s m c k"),
                                            axis=AX.X, op=ALU.add), r=[G_b], w=[tmpA_b])
                V(lambda e, Gl=Gl: e.tensor_reduce(out=sm[:, 32:40].rearrange("s (m h) -> s m h", m=2), in_=Gl,
                                                   axis=AX.X, op=ALU.add), r=[G_b], w=[sm_b])
                V(lambda e: e.tensor_tensor(out=acc[:].rearrange("s a c -> s (a c)"), in0=acc[:].rearrange("s a c -> s (a c)"),
                                            in1=tmpA[:], op=ALU.add), r=[acc_b, tmpA_b], w=[acc_b])
                V(lambda e: e.tensor_tensor(out=lsum[:].rearrange("s m h -> s (m h)"), in0=lsum[:].rearrange("s m h -> s (m h)"),
                                            in1=sm[:, 32:40], op=ALU.add), r=[lsum_b, sm_b], w=[lsum_b])
        V(lambda e: e.tensor_tensor(out=tmpA[:, 0:512], in0=zs[:, 0:512], in1=zs[:, 512:1024], op=ALU.mult),
          r=[zs_b], w=[tmpA_b])
        V(lambda e: e.tensor_reduce(out=sm[:, 0:8], in_=tmpA[:, 0:512].rearrange("s (x d) -> s x d", d=64),
                                    axis=AX.X, op=ALU.add), r=[tmpA_b], w=[sm_b])
        A(lambda e: e.activation(out=sm[:, 8:16], in_=sm[:, 0:8], func=AF.Exp, scale=ATT_SCALE), r=[sm_b], w=[sm_b])
        V(lambda e: e.tensor_tensor(out=lsum[:], in0=lsum[:], in1=sm[:, 8:16].rearrange("s (h m) -> s m h", m=2), op=ALU.add),
          r=[lsum_b, sm_b], w=[lsum_b])
        V(lambda e: e.tensor_tensor(
            out=tmpA[:].rearrange("s (m h c) -> s m h c", m=2, h=4),
            in0=zs[:, 1024:1536].rearrange("s (h c) -> s h c", h=4).unsqueeze(1).to_broadcast([NS, 2, 4, 128]),
            in1=sm[:, 8:16].rearrange("s (h m) -> s m h", m=2).unsqueeze(3).to_broadcast([NS, 2, 4, 128]),
            op=ALU.mult), r=[zs_b, sm_b], w=[tmpA_b])
        V(lambda e: e.tensor_tensor(out=acc[:].rearrange("s a c -> s (a c)"), in0=acc[:].rearrange("s a c -> s (a c)"),
                                    in1=tmpA[:], op=ALU.add), r=[acc_b, tmpA_b], w=[acc_b])
        V(lambda e: e.reciprocal(out=sm[:, 16:24], in_=lsum[:].rearrange("s m h -> s (m h)")), r=[lsum_b], w=[sm_b])
        V(lambda e: e.tensor_tensor(out=acc[:].rearrange("s m (h c) -> s (m h) c", h=4),
                                    in0=acc[:].rearrange("s m (h c) -> s (m h) c", h=4),
                                    in1=sm[:, 16:24].unsqueeze(2).to_broadcast([NS, 8, 128]),
                                    op=ALU.mult), r=[acc_b, sm_b], w=[acc_b])
        V(lambda e: e.scalar_tensor_tensor(out=att[:], in0=acc[:, 1, :], scalar=lam2[0:NS, 3:4], in1=acc[:, 0, :],
                                           op0=ALU.mult, op1=ALU.add), r=[acc_b, lam2_b], w=[att_b])
        if l == 0:
            dump(6, att[:], att_b, NS, 512)
            dump(7, acc[:].rearrange("s m c -> s (m c)"), acc_b, NS, 1024)
            dump(8, sm[:], sm_b, NS, 64)
        V(lambda e: e.tensor_tensor(out=tmpB[:, 0:512], in0=att[:], in1=att[:], op=ALU.mult), r=[att_b], w=[tmpB_b])
        V(lambda e: e.tensor_reduce(out=sm[:, 24:28], in_=tmpB[:, 0:512].rearrange("s (h c) -> s h c", h=4),
                                    axis=AX.X, op=ALU.add), r=[tmpB_b], w=[sm_b])
        V(lambda e: e.tensor_scalar(out=sm[:, 24:28], in0=sm[:, 24:28], scalar1=1.0 / 128, scalar2=LN_EPS,
                                    op0=ALU.mult, op1=ALU.add), r=[sm_b], w=[sm_b])
        A(lambda e: e.activation(out=sm[:, 24:28], in_=sm[:, 24:28], func=AF.Sqrt), r=[sm_b], w=[sm_b])
        V(lambda e: e.reciprocal(out=sm[:, 24:28], in_=sm[:, 24:28]), r=[sm_b], w=[sm_b])
        V(lambda e: e.tensor_tensor(out=att[:].rearrange("s (h c) -> s h c", h=4),
                                    in0=att[:].rearrange("s (h c) -> s h c", h=4),
                                    in1=sm[:, 24:28].unsqueeze(2).to_broadcast([NS, 4, 128]), op=ALU.mult),
          r=[att_b, sm_b], w=[att_b])
        V(lambda e: e.tensor_tensor(out=att[:].rearrange("s (h c) -> s h c", h=4),
                                    in0=att[:].rearrange("s (h c) -> s h c", h=4),
                                    in1=subg[0:NS, :].unsqueeze(1).to_broadcast([NS, 4, 128]), op=ALU.mult),
          r=[att_b, subg_b], w=[att_b])
        for gi, blk in enumerate((3, 7, 10)):
            A(lambda e, gi=gi, blk=blk: e.activation(out=sg[:, gi, :], in_=zs[:, blk * 512:(blk + 1) * 512], func=AF.Silu),
              r=[zs_b], w=[sg_b])
        V(lambda e: e.tensor_tensor(out=Y[:, 0, :], in0=att[:], in1=sg[:, 0, :], op=ALU.mult), r=[att_b, sg_b], w=[Y_b])
        zc = lambda blk: zs[:, blk * 512:(blk + 1) * 512]
        V(lambda e: e.tensor_tensor(out=convo[:, 1, :], in0=zc(5), in1=zc(6), op=ALU.mult), r=[zs_b], w=[convo_b])
        V(lambda e: e.tensor_copy(out=convo[:, 0, :], in_=stc_t[:, 1, :]), r=[stc_b], w=[convo_b])
        S.dma("sp", lambda e: e.dma_start(out=convs_o[l], in_=convo[:]), st_sem, r=[convo_b])
        V(lambda e: e.tensor_tensor(out=tmpB[:, 0:512], in0=stc_t[:, 0, :], in1=convw[0:NS, 0, :], op=ALU.mult),
          r=[stc_b, convw_b], w=[tmpB_b])
        V(lambda e: e.tensor_tensor(out=tmpB[:, 512:1024], in0=stc_t[:, 1, :], in1=convw[0:NS, 1, :], op=ALU.mult),
          r=[stc_b, convw_b], w=[tmpB_b])
        V(lambda e: e.tensor_tensor(out=tmpB[:, 0:512], in0=tmpB[:, 0:512], in1=tmpB[:, 512:1024], op=ALU.add),
          r=[tmpB_b], w=[tmpB_b])
        V(lambda e: e.tensor_tensor(out=tmpB[:, 512:1024], in0=convo[:, 1, :], in1=convw[0:NS, 2, :], op=ALU.mult),
          r=[convo_b, convw_b], w=[tmpB_b])
        V(lambda e: e.tensor_tensor(out=tmpB[:, 0:512], in0=tmpB[:, 0:512], in1=tmpB[:, 512:1024], op=ALU.add),
          r=[tmpB_b], w=[tmpB_b])
        V(lambda e: e.tensor_tensor(out=tmpB[:, 0:512], in0=tmpB[:, 0:512], in1=zc(4), op=ALU.mult),
          r=[tmpB_b, zs_b], w=[tmpB_b])
        V(lambda e: e.tensor_tensor(out=Y[:, 1, :], in0=tmpB[:, 0:512], in1=sg[:, 1, :], op=ALU.mult),
          r=[tmpB_b, sg_b], w=[Y_b])
        cvb = Buf("cv_view"); cvb.w = zs_b.w
        layer_norm_rows(zs[:, 9 * 512:10 * 512], zs_b, NS, 512, clng, clng_b, clnb, clnb_b, vn_s, vn_sb)
        S.dma("sp", lambda e: e.dma_start(out=cvs_o[l], in_=vn_s[:]), st_sem, r=[vn_sb])
        for g in range(4):
            V(lambda e, g=g: e.tensor_scalar(out=tmpB[:, g * 128:(g + 1) * 128], in0=vn_s[:, g * 128:(g + 1) * 128],
                                             scalar1=ws00[0:NS, g:g + 1], scalar2=bs0[0:NS, g:g + 1],
                                             op0=ALU.mult, op1=ALU.add), r=[vn_sb, ws00_b, bs0_b], w=[tmpB_b])
        V(lambda e: e.tensor_tensor(out=tmpB[:, 0:512], in0=tmpB[:, 0:512], in1=zc(8), op=ALU.mult),
          r=[tmpB_b, zs_b], w=[tmpB_b])
        V(lambda e: e.tensor_tensor(out=Y[:, 2, :], in0=tmpB[:, 0:512], in1=sg[:, 2, :], op=ALU.mult),
          r=[tmpB_b, sg_b], w=[Y_b])
        V(lambda e: e.tensor_copy(out=Yb[:], in_=Y[:].rearrange("s n c -> s (n c)")), r=[Y_b], w=[Yb_b])
        transpose_rows(Yb, Yb_b, YT, YT_b, 8, NS)
        for k in range(8, 12):
            S.op("pe", lambda e, k=k: e.transpose(out=PT[:, (k - 8) * NS:(k - 7) * NS],
                                                  in_=Yb[0:NS, k * 128:(k + 1) * 128], identity=identb[0:NS, 0:NS]),
                 r=[Yb_b, identb_b], w=[PTB])
        V(lambda e: e.tensor_copy(out=YT[:, 8:12, :], in_=PT[:, 0:4 * NS].rearrange("p (k r) -> p k r", r=NS)),
          r=[PTB], w=[YT_b])
        for n in range(3):
            A(lambda e, n=n: e.activation(out=zs[:, 5632 + n * D:5632 + (n + 1) * D], in_=zs[:, 5632 + n * D:5632 + (n + 1) * D],
                                          func=AF.Sigmoid), r=[zs_b], w=[zs_b])
        for n in range(3):
            for cb in range(2):
                wt, wb = wblock(wbr_d[l, n, :, cb * 512:(cb + 1) * 512], 4)
                ps, psb_ = next_ps()
                for k in range(4):
                    S.op("pe", lambda e, k=k, n=n, wt=wt, ps=ps: e.matmul(ps[0:NS, :], lhsT=YT[:, n * 4 + k, :], rhs=wt[:, k, :],
                                                                       start=(k == 0), stop=(k == 3)),
                         r=[YT_b, wb], w=[psb_])
                dst = merged[:, cb * 512:(cb + 1) * 512]
                gsl = zs[:, 5632 + n * D + cb * 512:5632 + n * D + (cb + 1) * 512]
                if n == 0:
                    V(lambda e, ps=ps, dst=dst, gsl=gsl: e.tensor_tensor(out=dst, in0=ps[0:NS, :], in1=gsl, op=ALU.mult),
                      r=[psb_, zs_b], w=[merged_b])
                else:
                    V(lambda e, ps=ps, gsl=gsl, cb=cb: e.tensor_tensor(out=tmpB[:, cb * 512:(cb + 1) * 512], in0=ps[0:NS, :], in1=gsl, op=ALU.mult),
                      r=[psb_, zs_b], w=[tmpB_b])
                    V(lambda e, dst=dst, cb=cb: e.tensor_tensor(out=dst, in0=dst, in1=tmpB[:, cb * 512:(cb + 1) * 512], op=ALU.add),
                      r=[tmpB_b, merged_b], w=[merged_b])
        V(lambda e: e.tensor_copy(out=mergb[:], in_=merged[:]), r=[merged_b], w=[mergb_b])
        transpose_rows(mergb, mergb_b, mT, mT_b, 8, NS)
        for cb in range(2):
            wt, wb = wblock(wout_d[l, :, cb * 512:(cb + 1) * 512], 8)
            ps, psb_ = next_ps()
            for k in range(8):
                S.op("pe", lambda e, k=k, wt=wt, ps=ps: e.matmul(ps[0:NS, :], lhsT=mT[:, k, :], rhs=wt[:, k, :],
                                                              start=(k == 0), stop=(k == 7)),
                     r=[mT_b, wb], w=[psb_])
            V(lambda e, ps=ps, cb=cb: e.tensor_tensor(out=res[:, cb * 512:(cb + 1) * 512], in0=ps[0:NS, :],
                                                      in1=mod_s[:, 2 * D + cb * 512:2 * D + (cb + 1) * 512], op=ALU.mult),
              r=[psb_, mod_b], w=[res_b])
        V(lambda e: e.scalar_tensor_tensor(out=res[:], in0=xs_t[:], scalar=ALPHA, in1=res[:], op0=ALU.mult, op1=ALU.add),
          r=[xs_b, res_b], w=[res_b])
        layer_norm_rows(res, res_b, NS, D, lng, lng_b, lnb, lnb_b, xs_t, xs_b)

    if do_sample:
        for l in range(DEPTH):
            sample_layer(l)
        S.dma("sp", lambda e: e.dma_start(out=ys_o, in_=xs_t[:]), st_sem, r=[xs_b])
    S.barrier()
    es_s.close()
    cur[0] = es

    def lvl(n):
        if plevel <= n:
            raise _Stop()

    if do_prompt and plevel > -1:
        S.op("pool", lambda e: e.tensor_copy(out=trilb[:], in_=trilf[:]), r=[trilf_b], w=[trilb_b])
        x1_d = dint("x1", [NT, D])
        kx_d = [dint(f"kx{l}", [128, 8192], BF16) for l in range(DEPTH)]
        vx_d = [dint(f"vx{l}", [128, 8192], BF16) for l in range(DEPTH)]
        gk_d = [dint(f"gk{l}", [256, 8192], BF16) for l in range(DEPTH)]
        gv_d = [dint(f"gv{l}", [256, 8192], BF16) for l in range(DEPTH)]
        xtl_d = dint("xtl", [32, 64]); gxt_d = dint("gxt", [64, 64])
        PAIRS = [[0, 1], [2, 3], [4, 5], [6, 7]]

        KTo = sb("KTo", [128, 4, NT], BF16); KTo_b = Buf("KTo")
        KTx = sb("KTx", [128, 4, NT], BF16); KTx_b = Buf("KTx")
        Vo = sb("Vo", [128, 16, 512], BF16); Vo_b = Buf("Vo")
        Vx = sb("Vx", [128, 16, 512], BF16); Vx_b = Buf("Vx")
        modp = sb("modp", [128, 3 * D]); modp_b = Buf("modp")
        cpc = sb("cpc", [128, 8]); cpc_b = Buf("cpc")
        cT_rep = sb("cT_rep", [128, 8, 128], BF16); cT_b = Buf("cT_rep")
        xt = sb("xt", [128, D]); xt_b = Buf("xt"); xt_sem = S.newsem("xt_s")
        ht = sb("ht", [128, D]); ht_b = Buf("ht")
        hb = sb("hb", [128, D], BF16); hb_b = Buf("hb")
        hT = sb("hT", [128, 8, 512], BF16); hT_b = Buf("hT")
        qT = sb("qT", [128, 4, 512], BF16); qT_b = Buf("qT")
        Pt = [sb(f"Pt{i}", [128, 512], BF16) for i in range(2)]; Pt_b = [Buf(f"Pt{i}") for i in range(2)]
        sqb, sqb_b = Pt[0], Pt_b[0]
        osb = [sb(f"osb{i}", [128, 512]) for i in range(3)]; osb_b = [Buf(f"osb{i}") for i in range(3)]
        yT = sb("yT", [128, 12, 512], BF16); yT_b = Buf("yT")
        WK = sb("WK", [128, 8, 512]); WK_b = [Buf(f"WK{i}") for i in range(8)]
        mrgT = sb("mrgT", [128, 8, 512], BF16); mrgT_b = Buf("mrgT")
        vn = sb("vn", [128, 4, 512], BF16); vn_b = Buf("vn")
        cvt = sb("cvt", [128, 512]); cvt_b = Buf("cvt")
        vnf = sb("vnf", [128, 512]); vnf_b = Buf("vnf")
        wsf, wsf_b = cvt[:, 0:128], cvt_b
        wsfb, wsfb_b = Pt[1][:, 0:128], Pt_b[1]
        wsT = sb("wsT", [128, 4, 128], BF16); wsT_b = Buf("wsT")
        bsb = sb("bsb", [128, 4, 128]); bsb_b = Buf("bsb")
        kst, kst_b = cvt, cvt_b; kst_sem = S.newsem("kst_s")
        vst, vst_b = vnf, vnf_b; vst_sem = S.newsem("vst_s")
        rr = sb("rr", [128, D]); rr_b = Buf("rr")
        hpb, hpb_b = hb, hb_b
        xpv_b = rr_b
        yo, yo_b = ht, ht_b; yo_sem = S.newsem("yo_s")
        cwc = sb("cwc", [128, 4, 3]); cwc_b = Buf("cwc", strict=True)
        gsc = sb("gsc", [128, 1]); gsc_b = Buf("gsc", strict=True)
        ucar = sb("ucar", [128, 4, 2]); ucar_b = Buf("ucar", strict=True)
        mbt = sb("mbt", [128, 1]); mbt_b = Buf("mbt")
        hft = sb("hft", [128, 1]); hft_b = Buf("hft")
        hTp = sb("hTp", [128, 8, 2], BF16); hTp_b = Buf("hTp")
        tmp2 = sb("tmp2", [128, 2]); tmp2_b = Buf("tmp2")
        pp_sem = S.newsem("pp_s")
        xch_sem = S.newsem("xch_s")
        x1_sem = S.newsem("x1_s")
        x1_b = Buf("x1")

        def ldp(t, b, src):
            S.dma("sp", lambda e: e.dma_start(out=t, in_=src), pp_sem, w=[b])

        p3 = [0]

        def next3():
            i = p3[0] % 3
            p3[0] += 1
            return PS[i], PSB[i]

        def fm_proj(wt, wb, nk, col0, rhs_of, rbufs, ncols=512):
            ps, pb = next3()
            for k in range(nk):
                S.op("pe", lambda e, k=k, ps=ps: e.matmul(ps[:, 0:ncols], lhsT=wt[:, k, col0:col0 + 128], rhs=rhs_of(k),
                                                        start=(k == 0), stop=(k == nk - 1)), r=[wb] + rbufs, w=[pb])
            return ps, pb

        ldp(mbt[:], mbt_b, mb_d)
        ldp(hft[:], hft_b, hf_d)
        ldp(cpc[:], cpc_b, cp_d.rearrange("o (k p) -> p (o k)", p=128))
        for b in (mbt_b, hft_b, cpc_b):
            b.w = (None, pp_sem[0], pp_sem[1])
        A(lambda e: e.activation(out=cpc[:], in_=cpc[:], func=AF.Silu), r=[cpc_b], w=[cpc_b])
        V(lambda e: e.tensor_copy(out=cT_rep[:], in_=cpc[:].unsqueeze(2).to_broadcast([128, 8, 128])), r=[cpc_b], w=[cT_b])

        def prompt_params(l):
            load_layer_params(l, sample=False)
            ldp(modp[:], modp_b, b_ada_d[l, :].partition_broadcast(128))
            for j in range(3):
                ldp(cwc[:, :, j], cwc_b, convw_d[l, j, :].rearrange("(c p) -> p c", p=128))
            ldp(gsc[:], gsc_b, subg_d[l, :].rearrange("(p o) -> p o", o=1))
            ldp(bsb[:].rearrange("p g t -> p (g t)"), bsb_b, cbs_d[l].rearrange("g t -> (g t)").partition_broadcast(128))
            for b in (modp_b, cwc_b, gsc_b, bsb_b):
                b.w = (None, pp_sem[0], pp_sem[1])
            V(lambda e: e.tensor_scalar(out=gsc[:], in0=gsc[:], scalar1=1.0 - lambda_init(l), scalar2=None, op0=ALU.mult),
              r=[gsc_b], w=[gsc_b])
            lvl(-0.5)
            for blk in range(6):
                wt, wb = wblock(w_ada_d[l, :, blk * 512:(blk + 1) * 512], 8)
                ps, pb = next3()
                for k in range(8):
                    S.op("pe", lambda e, k=k, wt=wt, ps=ps: e.matmul(ps[:, :], lhsT=cT_rep[:, k, :], rhs=wt[:, k, :],
                                                                  start=(k == 0), stop=(k == 7)), r=[cT_b, wb], w=[pb])
                V(lambda e, ps=ps, blk=blk: e.tensor_tensor(out=modp[:, blk * 512:(blk + 1) * 512], in0=ps[:, :],
                                                            in1=modp[:, blk * 512:(blk + 1) * 512], op=ALU.add),
                  r=[pb, modp_b], w=[modp_b])
            V(lambda e: e.tensor_scalar(out=modp[:, D:2 * D], in0=modp[:, D:2 * D], scalar1=1.0, scalar2=None, op0=ALU.add),
              r=[modp_b], w=[modp_b])
            lvl(-0.3)
            for g in range(4):
                S.dma("sp", lambda e, g=g: e.dma_start(out=wsf, in_=cws_d[l, g]), pp_sem, w=[wsf_b])
                V(lambda e: e.tensor_copy(out=wsfb, in_=wsf), r=[wsf_b], w=[wsfb_b])
                S.op("pe", lambda e: e.transpose(out=PT[:, 0:128], in_=wsfb, identity=identb[:]),
                     r=[wsfb_b, identb_b], w=[PTB])
                V(lambda e, g=g: e.tensor_tensor(out=wsT[:, g, :], in0=PT[:, 0:128], in1=trilb[:], op=ALU.mult),
                  r=[PTB, trilb_b], w=[wsT_b])

        def compute_hT(st, xsrc, xsrc_b):
            for tt in range(4):
                tok0 = st * 512 + tt * 128
                S.dma("sp", lambda e, tok0=tok0: e.dma_start(out=xt[:], in_=xsrc[tok0:tok0 + 128, :]), xt_sem,
                      r=[xsrc_b], w=[xt_b])
                V(lambda e: e.tensor_tensor(out=ht[:], in0=xt[:], in1=modp[:, D:2 * D], op=ALU.mult),
                  r=[xt_b, modp_b], w=[ht_b])
                V(lambda e: e.tensor_tensor(out=hb[:], in0=ht[:], in1=modp[:, 0:D], op=ALU.add),
                  r=[ht_b, modp_b], w=[hb_b])
                lvl(0.15)
                for k2 in range(2):
                    for kk in range(4):
                        k = k2 * 4 + kk
                        S.op("pe", lambda e, k=k, kk=kk: e.transpose(out=PT[:, kk * 128:(kk + 1) * 128], in_=hb[:, k * 128:(k + 1) * 128],
                                                                  identity=identb[:]), r=[hb_b, identb_b], w=[PTB])
                    V(lambda e, tt=tt, k2=k2: e.tensor_copy(out=hT[:, 4 * k2:4 * k2 + 4, tt * 128:(tt + 1) * 128],
                                                         in_=PT[:, 0:512].rearrange("p (k t) -> p k t", t=128)),
                      r=[PTB], w=[hT_b])
                lvl(0.18)

        def phase_a(l, xsrc, xsrc_b):
            Wk, Wk_b = wblock(w_in_d[l, :, 512:1024], 8)
            Wv, Wv_b = wblock(w_in_d[l, :, 1024:1536], 8)
            lvl(0.1)
            for st in range(4):
                compute_hT(st, xsrc, xsrc_b)
                lvl(0.2)
                for tt in range(4):
                    tok0 = st * 512 + tt * 128
                    ps, pb = next3()
                    for k in range(8):
                        S.op("pe", lambda e, k=k, ps=ps, tt=tt: e.matmul(ps[:, :], lhsT=hT[:, k, tt * 128:(tt + 1) * 128], rhs=Wk[:, k, :],
                                                                       start=(k == 0), stop=(k == 7)), r=[hT_b, Wk_b], w=[pb])
                    A(lambda e, ps=ps: e.copy(out=kst[:], in_=ps[:, :]), r=[pb], w=[kst_b])
                    S.dma("sp", lambda e, tok0=tok0: e.dma_start(out=kp_o[l, tok0:tok0 + 128, :], in_=kst[:]), kst_sem, r=[kst_b])
                    ps2, pb2 = next3()
                    for k in range(8):
                        S.op("pe", lambda e, k=k, ps2=ps2, tt=tt: e.matmul(ps2[:, :], lhsT=hT[:, k, tt * 128:(tt + 1) * 128], rhs=Wv[:, k, :],
                                                                         start=(k == 0), stop=(k == 7)), r=[hT_b, Wv_b], w=[pb2])
                    A(lambda e, ps2=ps2: e.copy(out=vst[:], in_=ps2[:, :]), r=[pb2], w=[vst_b])
                    S.dma("sp", lambda e, tok0=tok0: e.dma_start(out=vp_o[l, tok0:tok0 + 128, :], in_=vst[:]), vst_sem, r=[vst_b])
                    V(lambda e, st=st, tt=tt: e.tensor_copy(out=Vo[:, st * 4 + tt, :], in_=vst[:]), r=[vst_b], w=[Vo_b])
                lvl(0.3)
                for h in range(4):
                    ps, pb = fm_proj(Wk, Wk_b, 8, h * 128, lambda k: hT[:, k, :], [hT_b])
                    A(lambda e, ps=ps, h=h, st=st: e.copy(out=KTo[:, h, st * 512:(st + 1) * 512], in_=ps[:, :]), r=[pb], w=[KTo_b])
                lvl(0.4)
            lvl(0.5)
            kxb = Buf("kx"); vxb = Buf("vx"); gkb = Buf("gk"); gvb = Buf("gv")
            S.dma("sp", lambda e: e.dma_start(out=kx_d[l], in_=KTo[:].rearrange("p h t -> p (h t)")), xch_sem, r=[KTo_b], w=[kxb])
            S.dma("sp", lambda e: e.dma_start(out=vx_d[l], in_=Vo[:].rearrange("p a c -> p (a c)")), xch_sem, r=[Vo_b], w=[vxb])
            kxb.w = vxb.w = (None, xch_sem[0], xch_sem[1])
            if use_cc:
                S.cc(lambda e: e.collective_compute("AllGather", ALU.bypass, replica_groups=PAIRS, ins=[kx_d[l]], outs=[gk_d[l]]),
                     cc_sem, r=[kxb], w=[gkb])
                S.cc(lambda e: e.collective_compute("AllGather", ALU.bypass, replica_groups=PAIRS, ins=[vx_d[l]], outs=[gv_d[l]]),
                     cc_sem, r=[vxb], w=[gvb])
            else:
                S.dma("sp", lambda e: e.dma_start(out=gk_d[l][0:128, :], in_=kx_d[l]), xch_sem, r=[kxb], w=[gkb])
                S.dma("sp", lambda e: e.dma_start(out=gv_d[l][0:128, :], in_=vx_d[l]), xch_sem, r=[vxb], w=[gvb])
                gkb.w = gvb.w = (None, xch_sem[0], xch_sem[1])
            S.dma("sp", lambda e: e.dma_start(out=KTx[:].rearrange("p h t -> p (h t)"), in_=gk_d[l][0:128, :]), xch_sem, r=[gkb], w=[KTx_b])
            S.dma("sp", lambda e: e.dma_start(out=Vx[:].rearrange("p a c -> p (a c)"), in_=gv_d[l][0:128, :]), xch_sem, r=[gvb], w=[Vx_b])
            KTx_b.w = Vx_b.w = (None, xch_sem[0], xch_sem[1])

        def phase_b(l, xsrc, xsrc_b, xdst, xprev_src, xprev_b):
            blkW = lambda i: w_in_d[l, :, i * 512:(i + 1) * 512]
            S.dma("sp", lambda e: e.dma_start(out=rr[0:2, :], in_=xprev_src), pp_sem, r=[xprev_b], w=[xpv_b])
            xpv_b.w = (None, pp_sem[0], pp_sem[1])
            V(lambda e: e.tensor_tensor(out=rr[0:2, :], in0=rr[0:2, :], in1=modp[0:2, D:2 * D], op=ALU.mult), r=[xpv_b, modp_b], w=[xpv_b])
            V(lambda e: e.tensor_tensor(out=hpb[0:2, :], in0=rr[0:2, :], in1=modp[0:2, 0:D], op=ALU.add), r=[xpv_b, modp_b], w=[hpb_b])
            for k in range(8):
                S.op("pe", lambda e, k=k: e.transpose(out=PT[:, k * 2:(k + 1) * 2], in_=hpb[0:2, k * 128:(k + 1) * 128],
                                                      identity=identb[0:2, 0:2]), r=[hpb_b, identb_b], w=[PTB])
            V(lambda e: e.tensor_copy(out=hTp[:], in_=PT[:, 0:16].rearrange("p (k t) -> p k t", t=2)), r=[PTB], w=[hTp_b])
            for st in range(4):
                compute_hT(st, xsrc, xsrc_b)
                Wq, Wq_b = wblock(blkW(0), 8)
                for h in range(4):
                    ps, pb = fm_proj(Wq, Wq_b, 8, h * 128, lambda k: hT[:, k, :], [hT_b])
                    A(lambda e, ps=ps, h=h: e.copy(out=qT[:, h, :], in_=ps[:, :]), r=[pb], w=[qT_b])
                Wg, Wg_b = wblock(blkW(3), 8)
                for h in range(4):
                    ps, pb = fm_proj(Wg, Wg_b, 8, h * 128, lambda k: hT[:, k, :], [hT_b])
                    A(lambda e, ps=ps, h=h: e.activation(out=WK[:, 4 + h, :], in_=ps[:, :], func=AF.Silu), r=[pb], w=[WK_b[4 + h]])
                lvl(2)
                tiles = [("x", kt_, None) for kt_ in range(16)]
                for kt_ in range(st * 4 + 4):
                    tiles.append(("o", kt_, kt_ - st * 4 if kt_ >= st * 4 else None))
                cnt = 0
                for h in range(4):
                    for m in range(2):
                        ops_, opb = PS[6], PSB[6]
                        lps_, lpb = PS[3], PSB[3]
                        S.op("pe", lambda e, h=h: e.matmul(ops_[:, :], lhsT=zerosb[:], rhs=qT[:, h, :], start=True, stop=False),
                             r=[zerosb_b, qT_b], w=[opb])
                        S.op("pe", lambda e, h=h: e.matmul(lps_[:, :], lhsT=zerosb[:], rhs=qT[:, h, :], start=True, stop=False),
                             r=[zerosb_b, qT_b], w=[lpb])
                        for ti, (seg, kt_, dj) in enumerate(tiles):
                            last = (ti == len(tiles) - 1)
                            c0 = 0 if dj is None else dj * 128
                            KT, KT_b, VV, VV_b = (KTx, KTx_b, Vx, Vx_b) if seg == "x" else (KTo, KTo_b, Vo, Vo_b)
                            i = cnt % 2
                            cnt += 1
                            sps, spb = PS[4 + i], PSB[4 + i]
                            S.op("pe", lambda e, KT=KT, m=m, h=h, kt_=kt_, c0=c0, sps=sps: e.matmul(
                                sps[:, c0:512], lhsT=KT[m * 64:(m + 1) * 64, h, kt_ * 128:(kt_ + 1) * 128],
                                rhs=qT[m * 64:(m + 1) * 64, h, c0:512], start=True, stop=True), r=[KT_b, qT_b], w=[spb])
                            if seg == "x":
                                A(lambda e, i=i, sps=sps: e.activation(out=Pt[i][:, :], in_=sps[:, :], func=AF.Exp,
                                                                       scale=ATT_SCALE, bias=mbt[:, 0:1]), r=[spb, mbt_b], w=[Pt_b[i]])
                            else:
                                A(lambda e, i=i, sps=sps, c0=c0: e.activation(out=Pt[i][:, c0:512], in_=sps[:, c0:512], func=AF.Exp,
                                                                              scale=ATT_SCALE), r=[spb], w=[Pt_b[i]])
                            if dj is not None:
                                S.op("pool", lambda e, i=i, c0=c0: e.tensor_tensor(out=Pt[i][:, c0:c0 + 128], in0=Pt[i][:, c0:c0 + 128],
                                                                                    in1=trilb[:], op=ALU.mult),
                                     r=[Pt_b[i], trilb_b], w=[Pt_b[i]])
                            S.op("pe", lambda e, VV=VV, kt_=kt_, h=h, i=i, c0=c0, last=last: e.matmul(
                                ops_[:, c0:512], lhsT=VV[:, kt_, h * 128:(h + 1) * 128], rhs=Pt[i][:, c0:512],
                                start=False, stop=last), r=[VV_b, Pt_b[i]], w=[opb])
                            S.op("pe", lambda e, i=i, c0=c0, last=last: e.matmul(
                                lps_[:, c0:512], lhsT=onesb[:], rhs=Pt[i][:, c0:512], start=False, stop=last),
                                r=[onesb_b, Pt_b[i]], w=[lpb])
                        V(lambda e: e.reciprocal(out=osb[2][:], in_=lps_[:, :]), r=[lpb], w=[osb_b[2]])
                        V(lambda e, m=m: e.tensor_tensor(out=osb[m][:], in0=ops_[:, :], in1=osb[2][:], op=ALU.mult),
                          r=[opb, osb_b[2]], w=[osb_b[m]])
                    V(lambda e: e.scalar_tensor_tensor(out=osb[0][:], in0=osb[1][:], scalar=lam2[:, 3:4], in1=osb[0][:],
                                                       op0=ALU.mult, op1=ALU.add), r=[osb_b[0], osb_b[1], lam2_b], w=[osb_b[0]])
                    A(lambda e: e.activation(out=sqb[:], in_=osb[0][:], func=AF.Square), r=[osb_b[0]], w=[sqb_b])
                    ps, pb = next3()
                    S.op("pe", lambda e, ps=ps: e.matmul(ps[:, :], lhsT=onesb[:], rhs=sqb[:], start=True, stop=True),
                         r=[onesb_b, sqb_b], w=[pb])
                    V(lambda e, ps=ps: e.tensor_scalar(out=osb[2][:], in0=ps[:, :], scalar1=1.0 / 128, scalar2=LN_EPS,
                                                       op0=ALU.mult, op1=ALU.add), r=[pb], w=[osb_b[2]])
                    A(lambda e: e.activation(out=osb[2][:], in_=osb[2][:], func=AF.Sqrt), r=[osb_b[2]], w=[osb_b[2]])
                    V(lambda e: e.reciprocal(out=osb[2][:], in_=osb[2][:]), r=[osb_b[2]], w=[osb_b[2]])
                    V(lambda e: e.tensor_tensor(out=osb[0][:], in0=osb[0][:], in1=osb[2][:], op=ALU.mult),
                      r=[osb_b[0], osb_b[2]], w=[osb_b[0]])
                    V(lambda e, h=h: e.scalar_tensor_tensor(out=yT[:, h, :], in0=osb[0][:], scalar=gsc[:, 0:1], in1=WK[:, 4 + h, :],
                                                            op0=ALU.mult, op1=ALU.mult), r=[osb_b[0], gsc_b, WK_b[4 + h]], w=[yT_b])
                lvl(3)
                Wcc, Wcc_b = wblock(blkW(5), 8)
                Wcx, Wcx_b = wblock(blkW(6), 8)
                for c in range(4):
                    ps1, pb1 = fm_proj(Wcc, Wcc_b, 8, c * 128, lambda k: hT[:, k, :], [hT_b])
                    ps2, pb2 = fm_proj(Wcx, Wcx_b, 8, c * 128, lambda k: hT[:, k, :], [hT_b])
                    A(lambda e, ps2=ps2: e.copy(out=osb[2][:], in_=ps2[:, :]), r=[pb2], w=[osb_b[2]])
                    V(lambda e, ps1=ps1, c=c: e.tensor_tensor(out=WK[:, c, :], in0=ps1[:, :], in1=osb[2][:], op=ALU.mult),
                      r=[pb1, osb_b[2]], w=[WK_b[c]])
                    if st == 0:
                        ph1, phb1 = fm_proj(Wcc, Wcc_b, 8, c * 128, lambda k: hTp[:, k, :], [hTp_b], ncols=2)
                        ph2, phb2 = fm_proj(Wcx, Wcx_b, 8, c * 128, lambda k: hTp[:, k, :], [hTp_b], ncols=2)
                        A(lambda e, ph2=ph2: e.copy(out=tmp2[:], in_=ph2[:, 0:2]), r=[phb2], w=[tmp2_b])
                        V(lambda e, ph1=ph1, c=c: e.tensor_tensor(out=ucar[:, c, :], in0=ph1[:, 0:2], in1=tmp2[:], op=ALU.mult),
                          r=[phb1, tmp2_b], w=[ucar_b])
                        V(lambda e, c=c: e.tensor_scalar(out=ucar[:, c, :], in0=ucar[:, c, :], scalar1=hft[:, 0:1], scalar2=None,
                                                         op0=ALU.mult), r=[ucar_b, hft_b], w=[ucar_b])
                for c in range(4):
                    u_ = WK[:, c, :]
                    o_ = WK[:, 4 + c, :]
                    ub, ob = WK_b[c], WK_b[4 + c]
                    V(lambda e, u_=u_, o_=o_, c=c: e.tensor_scalar(out=o_, in0=u_, scalar1=cwc[:, c, 2:3], scalar2=None, op0=ALU.mult),
                      r=[ub, cwc_b], w=[ob])
                    V(lambda e, u_=u_, o_=o_, c=c: e.scalar_tensor_tensor(out=o_[:, 1:512], in0=u_[:, 0:511], scalar=cwc[:, c, 1:2],
                                                                          in1=o_[:, 1:512], op0=ALU.mult, op1=ALU.add),
                      r=[ub, ob, cwc_b], w=[ob])
                    V(lambda e, u_=u_, o_=o_, c=c: e.scalar_tensor_tensor(out=o_[:, 2:512], in0=u_[:, 0:510], scalar=cwc[:, c, 0:1],
                                                                          in1=o_[:, 2:512], op0=ALU.mult, op1=ALU.add),
                      r=[ub, ob, cwc_b], w=[ob])
                    V(lambda e, o_=o_, c=c: e.scalar_tensor_tensor(out=o_[:, 0:1], in0=ucar[:, c, 1:2], scalar=cwc[:, c, 1:2],
                                                                   in1=o_[:, 0:1], op0=ALU.mult, op1=ALU.add),
                      r=[ucar_b, ob, cwc_b], w=[ob])
                    V(lambda e, o_=o_, c=c: e.scalar_tensor_tensor(out=o_[:, 0:2], in0=ucar[:, c, 0:2], scalar=cwc[:, c, 0:1],
                                                                   in1=o_[:, 0:2], op0=ALU.mult, op1=ALU.add),
                      r=[ucar_b, ob, cwc_b], w=[ob])
                    V(lambda e, u_=u_, c=c: e.tensor_copy(out=ucar[:, c, :], in_=u_[:, 510:512]), r=[ub], w=[ucar_b])
                if st == 3:
                    for j in range(2):
                        S.dma("sp", lambda e, j=j: e.dma_start(out=convp_o[l, j, :].rearrange("(c p) -> p c", p=128), in_=ucar[:, :, j]),
                              st_sem, r=[ucar_b])
                Wcb, Wcb_b = wblock(blkW(4), 8)
                for c in range(4):
                    ps, pb = fm_proj(Wcb, Wcb_b, 8, c * 128, lambda k: hT[:, k, :], [hT_b])
                    V(lambda e, ps=ps, c=c: e.tensor_tensor(out=WK[:, 4 + c, :], in0=WK[:, 4 + c, :], in1=ps[:, :], op=ALU.mult),
                      r=[pb, WK_b[4 + c]], w=[WK_b[4 + c]])
                Wgc, Wgc_b = wblock(blkW(7), 8)
                for c in range(4):
                    ps, pb = fm_proj(Wgc, Wgc_b, 8, c * 128, lambda k: hT[:, k, :], [hT_b])
                    A(lambda e, ps=ps: e.activation(out=osb[2][:], in_=ps[:, :], func=AF.Silu), r=[pb], w=[osb_b[2]])
                    V(lambda e, c=c: e.tensor_tensor(out=yT[:, 4 + c, :], in0=WK[:, 4 + c, :], in1=osb[2][:], op=ALU.mult),
                      r=[WK_b[4 + c], osb_b[2]], w=[yT_b])
                lvl(4)
                Wcv, Wcv_b = wblock(blkW(9), 8)
                for tt in range(4):
                    ps, pb = next3()
                    for k in range(8):
                        S.op("pe", lambda e, k=k, ps=ps, tt=tt, Wcv=Wcv: e.matmul(ps[:, :], lhsT=hT[:, k, tt * 128:(tt + 1) * 128], rhs=Wcv[:, k, :],
                                                                                start=(k == 0), stop=(k == 7)), r=[hT_b, Wcv_b], w=[pb])
                    A(lambda e, ps=ps: e.copy(out=cvt[:], in_=ps[:, :]), r=[pb], w=[cvt_b])
                    layer_norm_rows(cvt, cvt_b, 128, 512, clng, clng_b, clnb, clnb_b, vnf, vnf_b, scr_t=ht, scr_b=ht_b)
                    V(lambda e, tt=tt: e.tensor_copy(out=vn[:, tt, :], in_=vnf[:]), r=[vnf_b], w=[vn_b])
                for g in range(4):
                    ps, pb = next3()
                    for tt in range(4):
                        S.op("pe", lambda e, ps=ps, tt=tt, g=g: e.matmul(ps[:, tt * 128:(tt + 1) * 128], lhsT=vn[:, tt, g * 128:(g + 1) * 128],
                                                                       rhs=wsT[:, g, :], start=True, stop=True), r=[vn_b, wsT_b], w=[pb])
                    V(lambda e, ps=ps, g=g: e.tensor_tensor(out=WK[:, g, :].rearrange("p (a t) -> p a t", t=128),
                                                            in0=ps[:, :].rearrange("p (a t) -> p a t", t=128),
                                                            in1=bsb[:, g, :].unsqueeze(1).to_broadcast([128, 4, 128]), op=ALU.add),
                      r=[pb, bsb_b], w=[WK_b[g]])
                if debug and l == 0 and st in (0, 1):
                    S.dma("sp", lambda e, st=st: e.dma_start(out=dbgv_o[st], in_=vn[:].rearrange("p a t -> p (a t)")), st_sem, r=[vn_b])
                    S.dma("sp", lambda e, st=st: e.dma_start(out=dbgx_o[st], in_=WK[:, 0:4, :].rearrange("p a t -> p (a t)")), st_sem, r=WK_b[0:4])
                Wcu, Wcu_b = wblock(blkW(8), 8)
                for g in range(4):
                    ps, pb = fm_proj(Wcu, Wcu_b, 8, g * 128, lambda k: hT[:, k, :], [hT_b])
                    V(lambda e, ps=ps, g=g: e.tensor_tensor(out=WK[:, g, :], in0=WK[:, g, :], in1=ps[:, :], op=ALU.mult),
                      r=[pb, WK_b[g]], w=[WK_b[g]])
                Wgk, Wgk_b = wblock(blkW(10), 8)
                for g in range(4):
                    ps, pb = fm_proj(Wgk, Wgk_b, 8, g * 128, lambda k: hT[:, k, :], [hT_b])
                    A(lambda e, ps=ps: e.activation(out=osb[2][:], in_=ps[:, :], func=AF.Silu), r=[pb], w=[osb_b[2]])
                    V(lambda e, g=g: e.tensor_tensor(out=yT[:, 8 + g, :], in0=WK[:, g, :], in1=osb[2][:], op=ALU.mult),
                      r=[WK_b[g], osb_b[2]], w=[yT_b])
                lvl(5)
                for n in range(3):
                    for cbk in range(2):
                        Wb, Wb_b = wblock(wbr_d[l, n, :, cbk * 512:(cbk + 1) * 512], 4)
                        c0w = 5632 + n * D + cbk * 512
                        Wm, Wm_b = wblock(w_in_d[l, :, c0w:c0w + 512], 8)
                        for j in range(4):
                            f = cbk * 4 + j
                            psg, pbg = fm_proj(Wm, Wm_b, 8, j * 128, lambda k: hT[:, k, :], [hT_b])
                            A(lambda e, psg=psg: e.activation(out=osb[2][:], in_=psg[:, :], func=AF.Sigmoid), r=[pbg], w=[osb_b[2]])
                            psp, pbp = fm_proj(Wb, Wb_b, 4, j * 128, lambda k, n=n: yT[:, n * 4 + k, :], [yT_b])
                            if n == 0:
                                V(lambda e, psp=psp, f=f: e.tensor_tensor(out=WK[:, f, :], in0=psp[:, :], in1=osb[2][:], op=ALU.mult),
                                  r=[pbp, osb_b[2]], w=[WK_b[f]])
                            else:
                                V(lambda e, psp=psp: e.tensor_tensor(out=osb[1][:], in0=psp[:, :], in1=osb[2][:], op=ALU.mult),
                                  r=[pbp, osb_b[2]], w=[osb_b[1]])
                                V(lambda e, f=f: e.tensor_tensor(out=WK[:, f, :], in0=WK[:, f, :], in1=osb[1][:], op=ALU.add),
                                  r=[WK_b[f], osb_b[1]], w=[WK_b[f]])
                if debug and l == 0 and st == 0:
                    S.dma("sp", lambda e: e.dma_start(out=dbgy_o, in_=yT[:].rearrange("p a t -> p (a t)")), st_sem, r=[yT_b])
                    S.dma("sp", lambda e: e.dma_start(out=dbgm_o, in_=WK[:].rearrange("p a t -> p (a t)")), st_sem, r=WK_b)
                for f in range(8):
                    A(lambda e, f=f: e.copy(out=mrgT[:, f, :], in_=WK[:, f, :]), r=[WK_b[f]], w=[mrgT_b])
                lvl(6)
                Wo0, Wo0_b = wblock(wout_d[l, :, 0:512], 8)
                Wo1, Wo1_b = wblock(wout_d[l, :, 512:1024], 8)
                for tt in range(4):
                    tok0 = st * 512 + tt * 128
                    S.dma("sp", lambda e, tok0=tok0: e.dma_start(out=xt[:], in_=xsrc[tok0:tok0 + 128, :]), xt_sem,
                          r=[xsrc_b], w=[xt_b])
                    for cb, (Wo, Wo_b) in enumerate(((Wo0, Wo0_b), (Wo1, Wo1_b))):
                        ps, pb = next3()
                        for k in range(8):
                            S.op("pe", lambda e, k=k, ps=ps, tt=tt, Wo=Wo: e.matmul(ps[:, :], lhsT=mrgT[:, k, tt * 128:(tt + 1) * 128],
                                                                                  rhs=Wo[:, k, :], start=(k == 0), stop=(k == 7)),
                                 r=[mrgT_b, Wo_b], w=[pb])
                        V(lambda e, ps=ps, cb=cb: e.tensor_tensor(out=rr[:, cb * 512:(cb + 1) * 512], in0=ps[:, :],
                                                                  in1=modp[:, 2 * D + cb * 512:2 * D + (cb + 1) * 512], op=ALU.mult),
                          r=[pb, modp_b], w=[rr_b])
                    V(lambda e: e.scalar_tensor_tensor(out=rr[:], in0=xt[:], scalar=ALPHA, in1=rr[:], op0=ALU.mult, op1=ALU.add),
                      r=[xt_b, rr_b], w=[rr_b])
                    layer_norm_rows(rr, rr_b, 128, D, lng, lng_b, lnb, lnb_b, yo, yo_b, scr_t=rr, scr_b=rr_b)
                    if l == 0:
                        S.dma("sp", lambda e, tok0=tok0: e.dma_start(out=xdst[tok0:tok0 + 128, :], in_=yo[:]), x1_sem, r=[yo_b])
                        yo_b.rd[id(x1_sem[0])] = (None, x1_sem[0], x1_sem[1])
                        if st == 3 and tt == 3:
                            S.dma("sp", lambda e: e.dma_start(out=xtl_d.rearrange("(r a) b -> r (a b)", r=2), in_=yo[126:128, :]),
                                  x1_sem, r=[yo_b])
                    else:
                        S.dma("sp", lambda e, tok0=tok0: e.dma_start(out=xdst[tok0:tok0 + 128, :], in_=yo[:]), yo_sem, r=[yo_b])

        def prompt_rest():
            x1_b.w = (None, x1_sem[0], x1_sem[1])
            xtl_b = Buf("xtl"); xtl_b.w = (None, x1_sem[0], x1_sem[1]); gxt_b = Buf("gxt")
            if use_cc:
                S.cc(lambda e: e.collective_compute("AllGather", ALU.bypass, replica_groups=PAIRS, ins=[xtl_d], outs=[gxt_d]),
                     cc_sem, r=[xtl_b], w=[gxt_b])
            else:
                S.dma("sp", lambda e: e.dma_start(out=gxt_d[0:32, :], in_=xtl_d), x1_sem, r=[xtl_b], w=[gxt_b])
            prompt_params(1)
            phase_a(1, x1_d, x1_b)
            phase_b(1, x1_d, x1_b, yp_o, gxt_d[0:32, :].rearrange("(r a) b -> r (a b)", r=2), gxt_b)

        x0_b = Buf("xp_in", const=True)
        xprev0_b = Buf("xprev_in", const=True)
        try:
            lvl(-0.7)
            prompt_params(0)
            lvl(0)
            phase_a(0, xp_d, x0_b)
            lvl(1)
            phase_b(0, xp_d, x0_b, x1_d, xprev_d, xprev0_b)
            lvl(8)
            prompt_rest()
        except _Stop:
            pass

    S.final_wait("sp", S.dsems)
    with nc.allow_non_contiguous_dma(reason="small strided parameter loads"), nc.Block() as block:
        S.emit(block)
    es.close()
    return nc


def make_consts():
    identf = np.eye(128, dtype=np.float32)
    identb = identf.astype(ml_dtypes.bfloat16)
    trilf = np.triu(np.ones((128, 128), np.float32))
    sel = np.zeros((NS, 16, 128), np.float32)
    for pr in range(16):
        sel[2 * pr, pr, 0:64] = 1.0
        sel[2 * pr + 1, pr, 64:128] = 1.0
    return dict(identf=identf, identb=identb, trilf=trilf, sel=sel)


def make_in_maps(inp, n_cores=8):
    cst = make_consts()
    f = lambda a: np.ascontiguousarray(np.asarray(a))
    ck = f(inp["cache_k"]); cv = f(inp["cache_v"])
    n_pool = ck.shape[1]
    ckr = ck.reshape(DEPTH, n_pool, 128, 512)
    cvr = cv.reshape(DEPTH, n_pool, 128, 512)
    shared = dict(
        xs=f(inp["x_sample"]).reshape(NS, D), cs=f(inp["c_sample"]), pt=f(inp["page_table"]).astype(np.int32),
        stc=f(inp["state_conv"]), w_ada=f(inp["w_ada"]), b_ada=f(inp["b_ada"]), w_in=f(inp["w_in"]),
        dlam=f(inp["diff_lambda"]).reshape(DEPTH, 256), subg=f(inp["subln_g"]), convw=f(inp["conv_w"]),
        clng=f(inp["chunk_ln_g"]), clnb=f(inp["chunk_ln_b"]), cws=f(inp["chunk_w_s"]), cbs=f(inp["chunk_b_s"]),
        wbr=f(inp["w_branch"]), wout=f(inp["w_out"]), lng=f(inp["ln_g"]), lnb=f(inp["ln_b"]), **cst)
    xp = f(inp["x_prompt"]); cp = f(inp["c_prompt"])
    maps = []
    for c in range(n_cores):
        b, half = c // 2, c % 2
        m = dict(shared)
        for l in range(DEPTH):
            for u in range(NU):
                r0 = ROWS * c + u * UR
                m[f"ck{l}_{u}"] = np.ascontiguousarray(ckr[l, :, r0:r0 + UR, :]).reshape(n_pool, UR * 512)
                m[f"cv{l}_{u}"] = np.ascontiguousarray(cvr[l, :, r0:r0 + UR, :]).reshape(n_pool, UR * 512)
        m["xp"] = np.ascontiguousarray(xp[b, half * NT:(half + 1) * NT])
        m["cp"] = np.ascontiguousarray(cp[b:b + 1])
        m["mb"] = np.full((128, 1), 0.0 if half == 1 else -30000.0, np.float32)
        m["hf"] = np.full((128, 1), 1.0 if half == 1 else 0.0, np.float32)
        m["xprev"] = np.ascontiguousarray(xp[b, NT - 2:NT]) if half == 1 else np.zeros((2, D), np.float32)
        maps.append(m)
    return maps, n_pool


def assemble(results):
    B, SEQ = 4, 4096
    yp = np.zeros((B, SEQ, D), np.float32)
    kp = np.zeros((DEPTH, B, SEQ, 512), np.float32)
    vp = np.zeros((DEPTH, B, SEQ, 512), np.float32)
    convp = np.zeros((DEPTH, B, 2, 512), np.float32)
    for c in range(8):
        b, half = c // 2, c % 2
        r = results[c]
        yp[b, half * NT:(half + 1) * NT] = r["yp"]
        kp[:, b, half * NT:(half + 1) * NT] = r["kp"]
        vp[:, b, half * NT:(half + 1) * NT] = r["vp"]
        if half == 1:
            convp[:, b] = r["convp"]
    r0 = results[0]
    return (yp, r0["ys"].reshape(NS, 1, D).copy(),
            kp.reshape(DEPTH, B, SEQ, 4, 2, 64), vp.reshape(DEPTH, B, SEQ, 4, 128), convp,
            r0["ks"].reshape(DEPTH, NS, 1, 4, 2, 64).copy(), r0["vs"].reshape(DEPTH, NS, 1, 4, 128).copy(),
            r0["convs"].copy(), r0["cvs"].reshape(DEPTH, NS, 1, 512).copy())


PROMPT_ENABLED = True


def kernel(**inputs):
    maps, n_pool = make_in_maps(inputs)
    nc = build_nc(n_pool=n_pool, do_prompt=PROMPT_ENABLED)
    res = run_bass_kernel_spmd(nc, maps, core_ids=list(range(8)))
    return assemble(res.results)
```
